# Optimizing a Trainium2 kernel written in Bass

```python
import math
import jax, jax.numpy as jnp
from jax import lax
import numpy as np

D_MODEL = 2048
BATCH = 4
SEQ = 4096
DEPTH = 2

HEAD_DIM = 128
SB_HEADS = D_MODEL // 256
DIFF_HEADS = D_MODEL // 512
DIFF_V_DIM = 2 * HEAD_DIM
SB_WIDTH = SB_HEADS * HEAD_DIM
DIFF_QK_WIDTH = DIFF_HEADS * 2 * HEAD_DIM
DIFF_V_WIDTH = DIFF_HEADS * DIFF_V_DIM
EVEN_IN_WIDTH = 3 * SB_WIDTH + 2 * DIFF_QK_WIDTH + DIFF_V_WIDTH
EVEN_MIX_WIDTH = SB_WIDTH + DIFF_V_WIDTH

MLA_HEADS = D_MODEL // 128
MLA_Q_RANK = 512
MLA_KV_RANK = 512
MLA_NOPE_DIM = 128
MLA_ROPE_DIM = 64
MLA_V_DIM = 128
MLA_QK_DIM = MLA_NOPE_DIM + MLA_ROPE_DIM
MLA_IN_WIDTH = MLA_Q_RANK + MLA_KV_RANK + MLA_ROPE_DIM

FFN_DIM = -(-8 * D_MODEL // (3 * 256)) * 256
ROPE_THETA = 10000.0
Q_BLOCK = 128
LN_EPS = 1e-5
RMS_EPS = 1e-6
DN_ALPHA = (2 * DEPTH) ** 0.25
DN_BETA = (8 * DEPTH) ** -0.25
N_EVEN = (DEPTH + 1) // 2
N_ODD = DEPTH // 2

kernel_name = "hybrid_stickbreak_diff_mla_deepnorm"


def layer_norm(x, g, b):
    xf = x.astype(jnp.float32)
    mu = jnp.mean(xf, axis=-1, keepdims=True)
    var = jnp.mean(jnp.square(xf - mu), axis=-1, keepdims=True)
    y = (xf - mu) * lax.rsqrt(var + LN_EPS) * g.astype(jnp.float32) + b.astype(jnp.float32)
    return y.astype(x.dtype)


def rms_norm(x, g, eps):
    xf = x.astype(jnp.float32)
    y = xf * lax.rsqrt(jnp.mean(jnp.square(xf), axis=-1, keepdims=True) + eps)
    return (y * g.astype(jnp.float32)).astype(x.dtype)


def rope_tables(positions, dim):
    inv_freq = ROPE_THETA ** (-jnp.arange(0, dim, 2, dtype=jnp.float32) / dim)
    ang = positions.astype(jnp.float32)[..., None] * inv_freq
    return jnp.cos(ang)[:, :, None, :], jnp.sin(ang)[:, :, None, :]


def apply_rope(x, cos, sin):
    x1, x2 = jnp.split(x.astype(jnp.float32), 2, axis=-1)
    return jnp.concatenate([x1 * cos - x2 * sin, x2 * cos + x1 * sin], axis=-1).astype(x.dtype)


def to_blocks(q):
    *lead, s, d = q.shape
    return jnp.moveaxis(q.reshape(*lead, s // Q_BLOCK, Q_BLOCK, d), -3, 0)


def from_blocks(o):
    o = jnp.moveaxis(o, 0, -3)
    *lead, nb, qb, d = o.shape
    return o.reshape(*lead, nb * qb, d)


def block_positions(i, seq_len):
    q_pos = i * Q_BLOCK + jnp.arange(Q_BLOCK)
    k_pos = jnp.arange(seq_len)
    return q_pos[:, None], k_pos[None, :]


def stick_breaking_attention(q, k, v):
    seq_len, d = q.shape[-2], q.shape[-1]
    scale = d ** -0.5

    def one_block(args):
        i, q_i = args
        qp, kp = block_positions(i, seq_len)
        strict = kp < qp
        z = jnp.einsum('bhqd,bhkd->bhqk', q_i, k).astype(jnp.float32) * scale
        log_beta = jax.nn.log_sigmoid(z)
        log_one_minus = jnp.where(strict, jax.nn.log_sigmoid(-z), 0.0)
        later = lax.cumsum(log_one_minus, axis=3, reverse=True) - log_one_minus
        w = jnp.where(strict, jnp.exp(log_beta + later), 0.0)
        return jnp.einsum('bhqk,bhkd->bhqd', w.astype(v.dtype), v)

    nb = seq_len // Q_BLOCK
    out = lax.map(one_block, (jnp.arange(nb), to_blocks(q)))
    return from_blocks(out)


def differential_attention(q, k, v, lam):
    seq_len, d = q.shape[-2], q.shape[-1]
    scale = d ** -0.5

    def one_block(args):
        i, q_i = args
        qp, kp = block_positions(i, seq_len)
        causal = kp <= qp
        s = jnp.einsum('bhmqd,bhmkd->bhmqk', q_i, k).astype(jnp.float32) * scale
        p = jax.nn.softmax(jnp.where(causal, s, -jnp.inf), axis=-1)
        a = p[:, :, 0] - lam * p[:, :, 1]
        return jnp.einsum('bhqk,bhkd->bhqd', a.astype(v.dtype), v)

    nb = seq_len // Q_BLOCK
    q_b = jnp.moveaxis(to_blocks(q), 0, 0)
    out = lax.map(one_block, (jnp.arange(nb), q_b))
    return from_blocks(out)


def causal_softmax_attention(q, k, v, scale):
    seq_len = q.shape[-2]

    def one_block(args):
        i, q_i = args
        qp, kp = block_positions(i, seq_len)
        s = jnp.einsum('bhqd,bhkd->bhqk', q_i, k).astype(jnp.float32) * scale
        p = jax.nn.softmax(jnp.where(kp <= qp, s, -jnp.inf), axis=-1)
        return jnp.einsum('bhqk,bhkd->bhqd', p.astype(v.dtype), v)

    nb = seq_len // Q_BLOCK
    out = lax.map(one_block, (jnp.arange(nb), to_blocks(q)))
    return from_blocks(out)


def sb_diff_mixer(x, cos_full, sin_full, w_in, w_out, lq1, lk1, lq2, lk2, subln_g, lambda_init):
    b, s, _ = x.shape
    h = x @ w_in
    cuts = list(np.cumsum([SB_WIDTH, SB_WIDTH, SB_WIDTH, DIFF_QK_WIDTH, DIFF_QK_WIDTH]))
    qa, ka, va, qd, kd, vd = jnp.split(h, cuts, axis=-1)

    def heads(t, n, d):
        return t.reshape(b, s, n, d).transpose(0, 2, 1, 3)

    oa = stick_breaking_attention(heads(qa, SB_HEADS, HEAD_DIM),
                                  heads(ka, SB_HEADS, HEAD_DIM),
                                  heads(va, SB_HEADS, HEAD_DIM))
    oa = oa.transpose(0, 2, 1, 3).reshape(b, s, SB_WIDTH)

    def diff_qk(t):
        t = apply_rope(t.reshape(b, s, 2 * DIFF_HEADS, HEAD_DIM), cos_full, sin_full)
        return t.reshape(b, s, DIFF_HEADS, 2, HEAD_DIM).transpose(0, 2, 3, 1, 4)

    lam = (jnp.exp(jnp.sum(lq1.astype(jnp.float32) * lk1.astype(jnp.float32)))
           - jnp.exp(jnp.sum(lq2.astype(jnp.float32) * lk2.astype(jnp.float32)))
           + lambda_init)
    od = differential_attention(diff_qk(qd), diff_qk(kd), heads(vd, DIFF_HEADS, DIFF_V_DIM), lam)
    od = rms_norm(od, subln_g, LN_EPS) * (1.0 - lambda_init)
    od = od.transpose(0, 2, 1, 3).reshape(b, s, DIFF_V_WIDTH)

    return jnp.concatenate([oa, od.astype(oa.dtype)], axis=-1) @ w_out


def mla_mixer(x, cos_rope, sin_rope, w_in, q_norm_g, kv_norm_g, w_q_up, w_kv_up, w_out):
    b, s, _ = x.shape
    h = x @ w_in
    c_q, c_kv, k_pe = jnp.split(h, [MLA_Q_RANK, MLA_Q_RANK + MLA_KV_RANK], axis=-1)
    q = (rms_norm(c_q, q_norm_g, RMS_EPS) @ w_q_up).reshape(b, s, MLA_HEADS, MLA_QK_DIM)
    q_nope, q_pe = jnp.split(q, [MLA_NOPE_DIM], axis=-1)
    q_pe = apply_rope(q_pe, cos_rope, sin_rope)
    kv = (rms_norm(c_kv, kv_norm_g, RMS_EPS) @ w_kv_up).reshape(b, s, MLA_HEADS, MLA_NOPE_DIM + MLA_V_DIM)
    k_nope, v = jnp.split(kv, [MLA_NOPE_DIM], axis=-1)
    k_pe = apply_rope(k_pe[:, :, None, :], cos_rope, sin_rope)
    k_pe = jnp.broadcast_to(k_pe, (b, s, MLA_HEADS, MLA_ROPE_DIM))
    qh = jnp.concatenate([q_nope, q_pe], axis=-1).transpose(0, 2, 1, 3)
    kh = jnp.concatenate([k_nope, k_pe], axis=-1).transpose(0, 2, 1, 3)
    o = causal_softmax_attention(qh, kh, v.transpose(0, 2, 1, 3), MLA_QK_DIM ** -0.5)
    return o.transpose(0, 2, 1, 3).reshape(b, s, MLA_HEADS * MLA_V_DIM) @ w_out


def swiglu_ffn(x, w_gate, w_up, w_down):
    return (jax.nn.silu(x @ w_gate) * (x @ w_up)) @ w_down


def setup_inputs(seed: int = 0) -> dict:
    key = jax.random.key(seed)
    ks = jax.random.split(key, 24)

    def w(k, shape, fan_in, gain=1.0):
        return jax.random.normal(k, shape, jnp.float32) * (gain * fan_in ** -0.5)

    def near_one(k, shape):
        return 1.0 + 0.02 * jax.random.normal(k, shape, jnp.float32)

    offsets = jax.random.randint(ks[1], (BATCH, 1), 0, 4096, dtype=jnp.int32)
    positions = (offsets + jnp.arange(SEQ, dtype=jnp.int32)[None, :]).astype(jnp.int32)

    return {
        "x": jax.random.normal(ks[0], (BATCH, SEQ, D_MODEL), jnp.float32),
        "positions": positions,
        "sb_diff_w_in": w(ks[2], (N_EVEN, D_MODEL, EVEN_IN_WIDTH), D_MODEL),
        "sb_diff_w_out": w(ks[3], (N_EVEN, EVEN_MIX_WIDTH, D_MODEL), EVEN_MIX_WIDTH, DN_BETA),
        "diff_lambda_q1": 0.1 * jax.random.normal(ks[4], (N_EVEN, HEAD_DIM), jnp.float32),
        "diff_lambda_k1": 0.1 * jax.random.normal(ks[5], (N_EVEN, HEAD_DIM), jnp.float32),
        "diff_lambda_q2": 0.1 * jax.random.normal(ks[6], (N_EVEN, HEAD_DIM), jnp.float32),
        "diff_lambda_k2": 0.1 * jax.random.normal(ks[7], (N_EVEN, HEAD_DIM), jnp.float32),
        "diff_subln_g": near_one(ks[8], (N_EVEN, DIFF_V_DIM)),
        "mla_w_in": w(ks[9], (N_ODD, D_MODEL, MLA_IN_WIDTH), D_MODEL),
        "mla_q_norm_g": near_one(ks[10], (N_ODD, MLA_Q_RANK)),
        "mla_kv_norm_g": near_one(ks[11], (N_ODD, MLA_KV_RANK)),
        "mla_w_q_up": w(ks[12], (N_ODD, MLA_Q_RANK, MLA_HEADS * MLA_QK_DIM), MLA_Q_RANK),
        "mla_w_kv_up": w(ks[13], (N_ODD, MLA_KV_RANK, MLA_HEADS * (MLA_NOPE_DIM + MLA_V_DIM)), MLA_KV_RANK),
        "mla_w_out": w(ks[14], (N_ODD, MLA_HEADS * MLA_V_DIM, D_MODEL), MLA_HEADS * MLA_V_DIM, DN_BETA),
        "ffn_w_gate": w(ks[15], (DEPTH, D_MODEL, FFN_DIM), D_MODEL),
        "ffn_w_up": w(ks[16], (DEPTH, D_MODEL, FFN_DIM), D_MODEL),
        "ffn_w_down": w(ks[17], (DEPTH, FFN_DIM, D_MODEL), FFN_DIM, DN_BETA),
        "ln_g": near_one(ks[18], (DEPTH, 2, D_MODEL)),
        "ln_b": 0.02 * jax.random.normal(ks[19], (DEPTH, 2, D_MODEL), jnp.float32),
    }


def reference(x, positions, sb_diff_w_in, sb_diff_w_out, diff_lambda_q1, diff_lambda_k1,
              diff_lambda_q2, diff_lambda_k2, diff_subln_g, mla_w_in, mla_q_norm_g,
              mla_kv_norm_g, mla_w_q_up, mla_w_kv_up, mla_w_out, ffn_w_gate, ffn_w_up,
              ffn_w_down, ln_g, ln_b):
    cos_full, sin_full = rope_tables(positions, HEAD_DIM)
    cos_rope, sin_rope = rope_tables(positions, MLA_ROPE_DIM)
    for layer in range(DEPTH):
        if layer % 2 == 0:
            i = layer // 2
            lambda_init = 0.8 - 0.6 * math.exp(-0.3 * layer)
            mix = sb_diff_mixer(x, cos_full, sin_full, sb_diff_w_in[i], sb_diff_w_out[i],
                                diff_lambda_q1[i], diff_lambda_k1[i], diff_lambda_q2[i],
                                diff_lambda_k2[i], diff_subln_g[i], lambda_init)
        else:
            j = layer // 2
            mix = mla_mixer(x, cos_rope, sin_rope, mla_w_in[j], mla_q_norm_g[j], mla_kv_norm_g[j],
                            mla_w_q_up[j], mla_w_kv_up[j], mla_w_out[j])
        x = layer_norm(DN_ALPHA * x + mix, ln_g[layer, 0], ln_b[layer, 0])
        x = layer_norm(DN_ALPHA * x + swiglu_ffn(x, ffn_w_gate[layer], ffn_w_up[layer], ffn_w_down[layer]),
                       ln_g[layer, 1], ln_b[layer, 1])
    return x
```

```python
import os
import math
from contextlib import ExitStack
import numpy as np
import ml_dtypes
import concourse.bass as bass
import concourse.mybir as mybir
from concourse.bass_utils import run_bass_kernel_spmd

F32 = mybir.dt.float32
BF16 = mybir.dt.bfloat16
I32 = mybir.dt.int32
AF = mybir.ActivationFunctionType
ALU = mybir.AluOpType
AX = mybir.AxisListType

D = 2048
T = 2048
NT = 16
NG = 4
S = 4096
FF = 5632
NFB = 11
ALPHA = 4.0 ** 0.25
LN_EPS = 1e-5
RMS_EPS = 1e-6
SC128 = 128.0 ** -0.5
SC192 = 192.0 ** -0.5
PI = math.pi


def gtile(r, j):
    return 2 * j + ((j & 1) ^ r)


def loc_of(g):
    j = g // 2
    r = (g & 1) ^ (j & 1)
    return r, j


class Res:
    __slots__ = ("name", "w", "r")

    def __init__(self, name=""):
        self.name = name
        self.w = None
        self.r = []


class Eng:
    def __init__(self, name, eng):
        self.name = name
        self.eng = eng
        self.seen = {}
        self.sem = None
        self.cnt = 0


class KB:
    LIMIT = int(os.environ.get('KLIMIT', '1000'))
    NRING = 8

    def __init__(self, nc):
        self.nc = nc
        self.nsem = 0
        self.allsems = []
        self.pe = Eng("pe", nc.tensor)
        self.act = Eng("act", nc.scalar)
        self.dve = Eng("dve", nc.vector)
        self.pool = Eng("pool", nc.gpsimd)
        self.sp = Eng("sp", nc.sync)
        self.engs = [self.pe, self.act, self.dve, self.pool, self.sp]
        for e in self.engs:
            e.sem = self._newsem(e.name)
        self.dq = {}
        for e in (self.sp, self.pool):
            self.dq[e.name] = {"ring": [[self._newsem(e.name + "d"), 0] for _ in range(self.NRING)], "i": 0}
        self.res = {}

    def R(self, *key):
        r = self.res.get(key)
        if r is None:
            r = Res(str(key))
            self.res[key] = r
        return r

    def _newsem(self, name):
        self.nsem += 1
        h = self.nc.alloc_semaphore(name="s%s%d" % (name, self.nsem))
        self.allsems.append(h)
        return h

    def _wait(self, e, deps):
        best = {}
        for (sem, cnt) in deps:
            k = id(sem)
            if k not in best or best[k][1] < cnt:
                best[k] = (sem, cnt)
        for k, (sem, cnt) in best.items():
            if e.seen.get(k, 0) >= cnt:
                continue
            e.eng.wait_ge(sem, cnt)
            e.seen[k] = cnt

    @staticmethod
    def _deps(reads, writes):
        deps = []
        for r in reads:
            if r.w is not None:
                deps.append(r.w)
        for w in writes:
            if w.w is not None:
                deps.append(w.w)
            deps.extend(w.r)
        return deps

    @staticmethod
    def _commit(tok, reads, writes):
        for r in reads:
            r.r.append(tok)
            if len(r.r) > 48:
                best = {}
                for (s, c) in r.r:
                    if id(s) not in best or best[id(s)][1] < c:
                        best[id(s)] = (s, c)
                r.r = list(best.values())
        for w in writes:
            w.w = tok
            w.r = []

    def op(self, e, reads, writes, fn):
        self._wait(e, self._deps(reads, writes))
        inst = fn()
        if e.cnt >= self.LIMIT:
            e.sem = self._newsem(e.name)
            e.cnt = 0
        e.cnt += 1
        inst.then_inc(e.sem, 1)
        tok = (e.sem, e.cnt)
        self._commit(tok, reads, writes)
        return tok

    def dma(self, e, reads, writes, fn):
        q = self.dq[e.name]
        slot = q["ring"][q["i"] % self.NRING]
        q["i"] += 1
        deps = self._deps(reads, writes)
        if slot[1] > 0:
            deps.append((slot[0], slot[1]))
        self._wait(e, deps)
        inst = fn()
        if slot[1] >= self.LIMIT:
            slot[0] = self._newsem(e.name + "d")
            slot[1] = 0
        slot[1] += 16
        inst.then_inc(slot[0], 16)
        tok = (slot[0], slot[1])
        self._commit(tok, reads, writes)
        return tok

    def barrier(self):
        deps = []
        for e in self.engs:
            if e.cnt > 0:
                deps.append((e.sem, e.cnt))
        for q in self.dq.values():
            for slot in q["ring"]:
                if slot[1] > 0:
                    deps.append((slot[0], slot[1]))
        for e in self.engs:
            self._wait(e, deps)

    def finish(self, res_list):
        deps = []
        for r in res_list:
            if r.w is not None:
                deps.append(r.w)
        self._wait(self.sp, deps)


def build(stop_after=99, debug=False):
    nc = bass.Bass("TRN2", target_bir_lowering=False)
    K = KB(nc)
    R = K.R

    need = {"x_own": 1, "pos": 1, "masks": 2, "cstb": 1, "colc": 1, "w0": 1, "w0r": 1, "wo0": 3, "lamv": 1, "subg": 1,
            "mwin": 4, "mwinr": 4, "mg": 4, "mwq": 5, "mwqr": 5, "mwkv": 5, "mwo": 6, "fg": 3, "fu": 3, "fd": 3,
            "lng": 3, "lnb": 3}
    used_inputs = []

    def din(name, shape, dt=F32):
        if need[name] > stop_after:
            return None
        used_inputs.append(name)
        return nc.dram_tensor(name, list(shape), dt, kind="ExternalInput").ap()

    def dscr(name, shape, dt=BF16):
        kind = "ExternalOutput" if (debug and name in DEBUG_OUT) else "Internal"
        return nc.dram_tensor(name, list(shape), dt, kind=kind).ap()

    DEBUG_OUT = ("qa", "qd", "att", "xa", "xb", "cq", "ropetab", "xT")

    x_own = din("x_own", [T, D])
    pos = din("pos", [1, T], I32)
    masks_d = din("masks", [128, 2, 8, 512], BF16)
    cstb_d = din("cstb", [128, 2, 128], BF16)
    colc_d = din("colc", [128, 8])
    w0 = din("w0", [D, 6144])
    w0r = din("w0r", [D, 2048])
    wo0 = din("wo0", [D, D])
    lamv_d = din("lamv", [1, 4 * 128])
    subg_d = din("subg", [128, 2])
    mwin = din("mwin", [D, 1088])
    mwinr = din("mwinr", [D, 64])
    mg_d = din("mg", [1, 1024])
    mwq = din("mwq", [512, 3072])
    mwqr = din("mwqr", [512, 1024])
    mwkv = din("mwkv", [512, 4096])
    mwo = din("mwo", [D, D])
    fg = din("fg", [2, D, FF])
    fu = din("fu", [2, D, FF])
    fd = din("fd", [2, FF, D])
    lng = din("lng", [4, D])
    lnb = din("lnb", [4, D])
    y_out = nc.dram_tensor("y", [T, D], F32, kind="ExternalOutput").ap()

    ropetab = dscr("ropetab", [128, 4, T], F32)
    qa_d = dscr("qa", [8, 128, T])
    qd_d = dscr("qd", [8, 128, T])
    kx_own = [dscr("kx_own%d" % i, [512, T]) for i in range(4)]
    kx_all = [dscr("kx_all%d" % i, [1024, T]) for i in range(4)]
    v_own = [dscr("v_own%d" % i, [512, 2048]) for i in range(4)]
    v_all = [dscr("v_all%d" % i, [1024, 2048]) for i in range(4)]
    att_d = dscr("att", [16, 128, T])
    xa_d = dscr("xa", [T, D], F32)
    xb_d = dscr("xb", [T, D], F32)
    xT_d = dscr("xT", [16, 128, T])
    cq_d = dscr("cq", [4, 128, T])
    mx_own = [dscr("mx_own0", [512, T]), dscr("mx_own1", [64, T])]
    mx_all = [dscr("mx_all0", [1024, T]), dscr("mx_all1", [128, T])]

    groups = [[0, 1], [2, 3], [4, 5], [6, 7]]
    wob = [dscr("wob%d" % l, [D, D]) for l in range(2)]
    fgb = [dscr("fgb%d" % l, [D, FF]) for l in range(2)]
    fub = [dscr("fub%d" % l, [D, FF]) for l in range(2)]
    fdb = [dscr("fdb%d" % l, [FF, D]) for l in range(2)]
    conv_units = [[], []]
    conv_res = [{"wo": [], "fg": [], "fu": [], "fd": []} for _ in range(2)]

    cstb = nc.alloc_sbuf_tensor("cstb_s", [128, 2, 128], BF16)
    ones = nc.alloc_sbuf_tensor("ones", [128, 128], BF16)
    negones = nc.alloc_sbuf_tensor("negones", [128, 128], BF16)
    zeros = nc.alloc_sbuf_tensor("zeros", [128, 128], BF16)
    colc = nc.alloc_sbuf_tensor("colc_s", [128, 8], F32)
    lamc = nc.alloc_sbuf_tensor("lamc", [128, 4], F32)
    subg = nc.alloc_sbuf_tensor("subg_s", [128, 2], F32)
    epsc = nc.alloc_sbuf_tensor("epsc", [128, 2], F32)
    ident = cstb[:, 0, :]
    uneg = cstb[:, 1, :]
    PS = [nc.alloc_psum_tensor("ps%d" % i, [128, 512], F32) for i in range(8)]
    RPS = [R("ps", i) for i in range(8)]
    r_const = R("const")

    pe, act, dve, pool, sp = K.pe, K.act, K.dve, K.pool, K.sp

    def mm(out, lhsT, rhs, start, stop):
        return nc.tensor.matmul(out, lhsT, rhs, start=start, stop=stop, skip_group_check=True)

    K.dma(sp, [], [r_const], lambda: nc.sync.dma_start(out=cstb[:], in_=cstb_d))
    K.dma(sp, [], [r_const], lambda: nc.sync.dma_start(out=colc[:], in_=colc_d))
    K.dma(sp, [], [r_const], lambda: nc.sync.dma_start(out=subg[:], in_=subg_d))
    K.op(pool, [], [r_const], lambda: nc.gpsimd.memset(ones[:], 1.0))
    K.op(pool, [], [r_const], lambda: nc.gpsimd.memset(negones[:], -1.0))
    K.op(pool, [], [r_const], lambda: nc.gpsimd.memset(zeros[:], 0.0))
    K.op(pool, [], [r_const], lambda: nc.gpsimd.memset(epsc[:, 0:1], LN_EPS))
    K.op(pool, [], [r_const], lambda: nc.gpsimd.memset(epsc[:, 1:2], RMS_EPS))
    K.op(dve, [r_const], [r_const], lambda: nc.vector.tensor_scalar(
        out=subg[:], in0=subg[:], scalar1=0.8, scalar2=None, op0=ALU.mult))

    with ExitStack() as es:
        posi = es.enter_context(nc.sbuf_tensor("p0a", [128, T], I32))
        posf = es.enter_context(nc.sbuf_tensor("p0b", [128, T], F32))
        ang = es.enter_context(nc.sbuf_tensor("p0c", [128, T], F32))
        tab = es.enter_context(nc.sbuf_tensor("p0d", [128, 4, T], F32))
        lv = es.enter_context(nc.sbuf_tensor("p0e", [128, 4, 128], F32))
        lprod = es.enter_context(nc.sbuf_tensor("p0f", [128, 128], F32))
        kf = es.enter_context(nc.sbuf_tensor("p0g", [128, T], F32))
        r_posi, r_posf, r_ang, r_tab, r_lv, r_lp = R("posi"), R("posf"), R("ang"), R("tab"), R("lv"), R("lprod")
        K.dma(sp, [], [r_posi], lambda: nc.sync.dma_start(out=posi[:], in_=pos.partition_broadcast(128)))
        K.op(dve, [r_posi], [r_posf], lambda: nc.vector.tensor_copy(out=posf[:], in_=posi[:]))
        C1 = 6.28125
        C2 = 2.0 * PI - 6.28125
        for ti, (fcol, scol, off) in enumerate([(0, None, 0.5 * PI), (0, 1, 0.0), (2, None, 0.5 * PI), (2, 3, 0.0)]):
            K.op(dve, [r_posf, r_const], [r_ang], lambda fcol=fcol, off=off: nc.vector.tensor_scalar(
                out=ang[:], in0=posf[:], scalar1=colc[:, fcol:fcol + 1], scalar2=off, op0=ALU.mult, op1=ALU.add))
            K.op(dve, [r_ang], [r_posi], lambda: nc.vector.tensor_scalar(
                out=posi[:], in0=ang[:], scalar1=1.0 / (2.0 * PI), scalar2=None, op0=ALU.mult))
            K.op(dve, [r_posi], [R("kf")], lambda: nc.vector.tensor_copy(out=kf[:], in_=posi[:]))
            K.op(dve, [R("kf"), r_ang], [r_ang], lambda: nc.vector.scalar_tensor_tensor(
                out=ang[:], in0=kf[:], scalar=-C1, in1=ang[:], op0=ALU.mult, op1=ALU.add))
            K.op(dve, [R("kf"), r_ang], [r_ang], lambda: nc.vector.scalar_tensor_tensor(
                out=ang[:], in0=kf[:], scalar=-C2, in1=ang[:], op0=ALU.mult, op1=ALU.add))
            K.op(dve, [r_ang], [r_ang], lambda: nc.vector.tensor_scalar(
                out=ang[:], in0=ang[:], scalar1=-PI, scalar2=PI, op0=ALU.max, op1=ALU.min))
            K.op(act, [r_ang], [r_tab], lambda ti=ti: nc.scalar.activation(
                out=tab[:, ti, :], in_=ang[:], func=AF.Sin))
            if scol is not None:
                K.op(dve, [r_tab, r_const], [r_tab], lambda ti=ti, scol=scol: nc.vector.tensor_scalar(
                    out=tab[:, ti, :], in0=tab[:, ti, :], scalar1=colc[:, scol:scol + 1], scalar2=None, op0=ALU.mult))
        r_ropetab = R("ropetab")
        K.dma(sp, [r_tab], [r_ropetab], lambda: nc.sync.dma_start(out=ropetab, in_=tab[:]))
        K.dma(sp, [], [r_lv], lambda: nc.sync.dma_start(
            out=lv[:].rearrange("p a b -> p (a b)"), in_=lamv_d.partition_broadcast(128)))
        for i in range(2):
            K.op(dve, [r_lv], [r_lp], lambda i=i: nc.vector.tensor_tensor(
                out=lprod[:], in0=lv[:, 2 * i, :], in1=lv[:, 2 * i + 1, :], op=ALU.mult))
            K.op(dve, [r_lp], [r_const], lambda i=i: nc.vector.reduce_sum(
                out=lamc[:, 1 + i:2 + i], in_=lprod[:], axis=AX.X))
        K.op(act, [r_const], [r_const], lambda: nc.scalar.activation(out=lamc[:, 1:3], in_=lamc[:, 1:3], func=AF.Exp))
        K.op(dve, [r_const], [r_const], lambda: nc.vector.tensor_tensor(
            out=lamc[:, 0:1], in0=lamc[:, 2:3], in1=lamc[:, 1:2], op=ALU.subtract))
        K.op(dve, [r_const], [r_const], lambda: nc.vector.tensor_scalar(
            out=lamc[:, 0:1], in0=lamc[:, 0:1], scalar1=-0.2, scalar2=None, op0=ALU.add))
        K.barrier()

    def conv_setup(layer, wo_src):
        if wo_src is None or fg is None:
            return
        u = conv_units[layer]
        for i in range(4):
            rr_ = R("cv", layer, "wo", i)
            conv_res[layer]["wo"].append(rr_)
            u.append((wo_src[i * 512:(i + 1) * 512, :], wob[layer][i * 512:(i + 1) * 512, :], rr_))
        for i in range(16):
            for nm, src, dst in (("fg", fg, fgb), ("fu", fu, fub)):
                rr_ = R("cv", layer, nm, i)
                conv_res[layer][nm].append(rr_)
                u.append((src[layer][i * 128:(i + 1) * 128, :], dst[layer][i * 128:(i + 1) * 128, :], rr_))
        for i in range(8):
            rr_ = R("cv", layer, "fd", i)
            conv_res[layer]["fd"].append(rr_)
            u.append((fd[layer][i * 704:(i + 1) * 704, :], fdb[layer][i * 704:(i + 1) * 704, :], rr_))

    def conv_step(layer, n=1):
        for _ in range(n):
            if not conv_units[layer]:
                return
            src, dst, rr_ = conv_units[layer].pop(0)
            K.dma(pool, [], [rr_], lambda src=src, dst=dst: nc.gpsimd.dma_start(out=dst, in_=src))

    def conv_flush(layer):
        conv_step(layer, len(conv_units[layer]))

    conv_setup(0, wo0)
    conv_setup(1, mwo)

    psrot = [0]

    def next_ps():
        i = psrot[0] % 8
        psrot[0] += 1
        return i

    def load_wblock(dst, rdst, wsrc, col0, ncols, nchunks=16, row0=0, deps=()):
        src = wsrc[row0:row0 + nchunks * 128, col0:col0 + ncols].rearrange("(c p) n -> p c n", p=128)
        K.dma(pool, list(deps), [rdst], lambda: nc.gpsimd.dma_start(out=dst[:, 0:nchunks, 0:ncols], in_=src))

    def transpose_rows(src_bf, r_src, dstT, r_dst, tcol0):
        for c4 in range(4):
            pi = next_ps()
            psb = PS[pi][:].bitcast(BF16)
            def f(c4=c4, psb=psb):
                ins = None
                for k in range(4):
                    c = c4 * 4 + k
                    ins = nc.tensor.transpose(psb[:, k * 128:(k + 1) * 128], src_bf[:, c * 128:(c + 1) * 128], ident)
                return ins
            K.op(pe, [r_src, r_const], [RPS[pi]], f)
            eng = dve if c4 % 2 == 0 else act
            if eng is dve:
                K.op(dve, [RPS[pi]], [r_dst], lambda c4=c4, psb=psb: nc.vector.tensor_copy(
                    out=dstT[:, c4 * 4:(c4 + 1) * 4, tcol0:tcol0 + 128],
                    in_=psb[:, 0:512].rearrange("p (k n) -> p k n", k=4)))
            else:
                K.op(act, [RPS[pi]], [r_dst], lambda c4=c4, psb=psb: nc.scalar.copy(
                    out=dstT[:, c4 * 4:(c4 + 1) * 4, tcol0:tcol0 + 128],
                    in_=psb[:, 0:512].rearrange("p (k n) -> p k n", k=4)))

    def phase_p1():
        with ExitStack() as es:
            XT = es.enter_context(nc.sbuf_tensor("XT", [128, 16, T], BF16))
            tab1 = es.enter_context(nc.sbuf_tensor("tab1", [128, 2, T], F32))
            wr = es.enter_context(nc.sbuf_tensor("wr", [128, 3, 16, 512], BF16))
            stg = es.enter_context(nc.sbuf_tensor("stg", [128, 2, T], BF16))
            xin = es.enter_context(nc.sbuf_tensor("xin", [128, 2, D], F32))
            xb16 = es.enter_context(nc.sbuf_tensor("xb16", [128, 2, D], BF16))
            t12 = es.enter_context(nc.sbuf_tensor("t12", [128, 4, 512], F32))
            vst = es.enter_context(nc.sbuf_tensor("vst", [128, 2, 512], BF16))
            r_XT = [R("XT", j) for j in range(NT)]
            r_tab1 = R("tab1")
            K.dma(sp, [R("ropetab")], [r_tab1], lambda: nc.sync.dma_start(out=tab1[:], in_=ropetab[:, 0:2, :]))
            for j in range(NT):
                b = j % 2
                K.dma(sp, [], [R("xin", b)], lambda j=j, b=b: nc.sync.dma_start(
                    out=xin[:, b, :], in_=x_own[j * 128:(j + 1) * 128, :]))
                K.op(pool, [R("xin", b)], [R("xb16", b)], lambda b=b: nc.gpsimd.tensor_copy(
                    out=xb16[:, b, :], in_=xin[:, b, :]))
                transpose_rows(xb16[:, b, :], R("xb16", b), XT, r_XT[j], j * 128)
            wi = [0]

            def wslot():
                s = wi[0] % 3
                wi[0] += 1
                return s

            si = [0]
            blocks = []
            for hb in range(2):
                blocks.append(("qa", hb * 512, None, hb, SC128))
            for hb in range(2):
                blocks.append(("ka", 1024 + hb * 512, None, hb, 1.0))
            for hb in range(2):
                blocks.append(("qd", 3072 + hb * 512, hb * 512, hb, SC128))
            for hb in range(2):
                blocks.append(("kd", 4096 + hb * 512, 1024 + hb * 512, hb, 1.0))
            for (kind, c0, cr, hb, scale) in blocks:
                sa = wslot()
                load_wblock(wr[:, sa], R("wr", sa), w0, c0, 512)
                if cr is not None:
                    sb = wslot()
                    load_wblock(wr[:, sb], R("wr", sb), w0r, cr, 512)
                for k in range(4):
                    ch = hb * 4 + k
                    st = si[0] % 2
                    si[0] += 1
                    r_st = R("stg", st)
                    for tg in range(NG):
                        pa = next_ps()
                        def fa(sa=sa, k=k, tg=tg, pa=pa):
                            ins = None
                            for c in range(16):
                                ins = mm(PS[pa][:], wr[:, sa, c, k * 128:(k + 1) * 128], XT[:, c, tg * 512:(tg + 1) * 512],
                                         c == 0, c == 15)
                            return ins
                        K.op(pe, [R("wr", sa)] + r_XT[tg * 4:tg * 4 + 4], [RPS[pa]], fa)
                        if cr is None:
                            if tg % 2 == 0:
                                K.op(act, [RPS[pa]], [r_st], lambda st=st, tg=tg, pa=pa, scale=scale: nc.scalar.mul(
                                    out=stg[:, st, tg * 512:(tg + 1) * 512], in_=PS[pa][:], mul=scale))
                            else:
                                K.op(dve, [RPS[pa]], [r_st], lambda st=st, tg=tg, pa=pa, scale=scale: nc.vector.tensor_scalar(
                                    out=stg[:, st, tg * 512:(tg + 1) * 512], in0=PS[pa][:], scalar1=scale, scalar2=None,
                                    op0=ALU.mult))
                        else:
                            pb = next_ps()
                            def fb(sb=sb, k=k, tg=tg, pb=pb):
                                ins = None
                                for c in range(16):
                                    ins = mm(PS[pb][:], wr[:, sb, c, k * 128:(k + 1) * 128], XT[:, c, tg * 512:(tg + 1) * 512],
                                             c == 0, c == 15)
                                return ins
                            K.op(pe, [R("wr", sb)] + r_XT[tg * 4:tg * 4 + 4], [RPS[pb]], fb)
                            ta = (tg % 2) * 2
                            K.op(dve, [RPS[pa], r_tab1], [R("t12", ta)], lambda ta=ta, tg=tg, pa=pa: nc.vector.tensor_tensor(
                                out=t12[:, ta, :], in0=PS[pa][:], in1=tab1[:, 0, tg * 512:(tg + 1) * 512], op=ALU.mult))
                            K.op(dve, [RPS[pb], r_tab1], [R("t12", ta + 1)], lambda ta=ta, tg=tg, pb=pb: nc.vector.tensor_tensor(
                                out=t12[:, ta + 1, :], in0=PS[pb][:], in1=tab1[:, 1, tg * 512:(tg + 1) * 512], op=ALU.mult))
                            K.op(pool, [R("t12", ta), R("t12", ta + 1)], [R("t12", ta)], lambda ta=ta: nc.gpsimd.tensor_tensor(
                                out=t12[:, ta, :], in0=t12[:, ta, :], in1=t12[:, ta + 1, :], op=ALU.add))
                            K.op(act, [R("t12", ta)], [r_st], lambda st=st, tg=tg, ta=ta, scale=scale: nc.scalar.mul(
                                out=stg[:, st, tg * 512:(tg + 1) * 512], in_=t12[:, ta, :], mul=scale))
                    if kind == "qa":
                        K.dma(sp, [r_st], [R("qa_d", ch)], lambda st=st, ch=ch: nc.sync.dma_start(out=qa_d[ch], in_=stg[:, st, :]))
                    elif kind == "qd":
                        K.dma(sp, [r_st], [R("qd_d", ch)], lambda st=st, ch=ch: nc.sync.dma_start(out=qd_d[ch], in_=stg[:, st, :]))
                    elif kind == "ka":
                        K.dma(sp, [r_st], [R("kx_own", ch)], lambda st=st, ch=ch: nc.sync.dma_start(
                            out=kx_own[ch // 4][(ch % 4) * 128:(ch % 4 + 1) * 128, :], in_=stg[:, st, :]))
                    else:
                        K.dma(sp, [r_st], [R("kx_own", 8 + ch)], lambda st=st, ch=ch: nc.sync.dma_start(
                            out=kx_own[2 + ch // 4][(ch % 4) * 128:(ch % 4 + 1) * 128, :], in_=stg[:, st, :]))
            vi = [0]
            for nb, c0 in enumerate([2048, 2560, 5120, 5632]):
                sa = wslot()
                load_wblock(wr[:, sa], R("wr", sa), w0, c0, 512)
                for j in range(NT):
                    pa = next_ps()
                    def fv(sa=sa, j=j, pa=pa):
                        ins = None
                        for c in range(16):
                            ins = mm(PS[pa][:], XT[:, c, j * 128:(j + 1) * 128], wr[:, sa, c, :], c == 0, c == 15)
                        return ins
                    K.op(pe, [R("wr", sa), r_XT[j]], [RPS[pa]], fv)
                    vs = vi[0] % 2
                    vi[0] += 1
                    if vs == 0:
                        K.op(act, [RPS[pa]], [R("vst", vs)], lambda vs=vs, pa=pa: nc.scalar.copy(out=vst[:, vs, :], in_=PS[pa][:]))
                    else:
                        K.op(dve, [RPS[pa]], [R("vst", vs)], lambda vs=vs, pa=pa: nc.vector.tensor_copy(out=vst[:, vs, :], in_=PS[pa][:]))
                    K.dma(sp, [R("vst", vs)], [R("v_own", nb, j)], lambda vs=vs, j=j, nb=nb: nc.sync.dma_start(
                        out=v_own[j // 4][(j % 4) * 128:(j % 4 + 1) * 128, nb * 512:(nb + 1) * 512], in_=vst[:, vs, :]))
            for i in range(4):
                K.op(pool, [R("kx_own", 4 * i + k) for k in range(4)], [R("kx_all", i)], lambda i=i: nc.gpsimd.collective_compute(
                    "AllGather", ALU.bypass, replica_groups=groups, ins=[kx_own[i].opt()], outs=[kx_all[i].opt()]))
            for i in range(4):
                K.op(pool, [R("v_own", nb, j) for nb in range(4) for j in range(4 * i, 4 * i + 4)], [R("v_all", i)],
                     lambda i=i: nc.gpsimd.collective_compute(
                         "AllGather", ALU.bypass, replica_groups=groups, ins=[v_own[i].opt()], outs=[v_all[i].opt()]))
            K.barrier()

    def key_iter(G):
        out = []
        for kt in range(8 * G + 8):
            rel = kt - 8 * G
            col0 = 128 * (rel // 2) if rel >= 0 else 0
            r, j = loc_of(kt)
            out.append((kt, col0, rel if rel >= 0 else None, r, j))
        return out

    def phase_p2():
        with ExitStack() as es:
            msk = es.enter_context(nc.sbuf_tensor("msk", [128, 2, 8, 512], BF16))
            qh = es.enter_context(nc.sbuf_tensor("qh", [128, 2, 2, T], BF16))
            kh = es.enter_context(nc.sbuf_tensor("kh", [128, 2, 2, 2 * T], BF16))
            vh = es.enter_context(nc.sbuf_tensor("vh", [128, 2, 32, 256], BF16))
            wE = es.enter_context(nc.sbuf_tensor("wE", [128, 3, 512], F32))
            wSPf = es.enter_context(nc.sbuf_tensor("wSPf", [128, 2, 512], F32))
            wSP = es.enter_context(nc.sbuf_tensor("wSP", [128, 3, 512], BF16))
            wTMP = es.enter_context(nc.sbuf_tensor("wTMP", [128, 3, 512], F32))
            wW = es.enter_context(nc.sbuf_tensor("wW", [128, 4, 512], BF16))
            carry = es.enter_context(nc.sbuf_tensor("carry", [128, 512], F32))
            ost = es.enter_context(nc.sbuf_tensor("ost", [128, 2, 512], BF16))
            fin = es.enter_context(nc.sbuf_tensor("fin", [128, 8, 512], F32))
            sqb = es.enter_context(nc.sbuf_tensor("sqb", [128, 2, 512], BF16))
            accd = es.enter_context(nc.sbuf_tensor("accd", [128, 2, 512], F32))
            hld = es.enter_context(nc.sbuf_tensor("hld", [128, 2, 2, 512], BF16))
            r_msk = R("msk")
            K.dma(sp, [], [r_msk], lambda: nc.sync.dma_start(out=msk[:], in_=masks_d))
            oi = [0]
            MINI = os.environ.get("KMINI", "")
            for h in range(8 if not MINI else (1 if "s" in MINI else 0)):
                hb = h % 2
                r_q, r_k, r_v = R("qh", hb), R("kh", hb), R("vh", hb)
                K.dma(sp, [R("qa_d", h)], [r_q], lambda h=h, hb=hb: nc.sync.dma_start(out=qh[:, hb, 0, :], in_=qa_d[h]))
                for rr_ in range(2):
                    K.dma(sp, [R("kx_all", h // 4)], [r_k], lambda h=h, hb=hb, rr_=rr_: nc.sync.dma_start(
                        out=kh[:, hb, 0, rr_ * T:(rr_ + 1) * T],
                        in_=kx_all[h // 4][rr_ * 512 + (h % 4) * 128:rr_ * 512 + (h % 4 + 1) * 128, :]))
                    for i in range(4):
                        K.dma(sp, [R("v_all", i)], [r_v], lambda h=h, hb=hb, rr_=rr_, i=i: nc.sync.dma_start(
                            out=vh[:, hb, rr_ * 16 + i * 4:rr_ * 16 + i * 4 + 4, 0:128],
                            in_=v_all[i][rr_ * 512:(rr_ + 1) * 512, h * 128:(h + 1) * 128].rearrange("(l p) d -> p l d", p=128)))
                for G in range(NG):
                    conv_step(0)
                    keys = key_iter(G)[::-1]
                    n = len(keys)
                    r_carry = R("carry")
                    K.op(pool, [], [r_carry], lambda: nc.gpsimd.memset(carry[:], 0.0))
                    OB = 7
                    K.op(pe, [r_const, r_q], [RPS[OB]], lambda hb=hb, G=G: mm(
                        PS[OB][:], zeros[:], qh[:, hb, 0, G * 512:(G + 1) * 512], True, False))
                    for it in range(n + 2):
                        if it < n:
                            kt, col0, rel, r, j = keys[it]
                            zb = it % 3
                            cs = slice(col0, 512)
                            kcol = r * T + j * 128
                            K.op(pe, [r_k, r_q], [RPS[zb]], lambda hb=hb, G=G, zb=zb, cs=cs, kcol=kcol, col0=col0: mm(
                                PS[zb][:, cs], kh[:, hb, 0, kcol:kcol + 128], qh[:, hb, 0, G * 512 + col0:(G + 1) * 512], True, False))
                            K.op(act, [RPS[zb]], [R("wE", zb)], lambda zb=zb, cs=cs: nc.scalar.activation(
                                out=wE[:, zb, cs], in_=PS[zb][:, cs], func=AF.Exp))
                            if rel is None:
                                K.op(act, [R("wE", zb)], [R("wSP", zb)], lambda zb=zb, cs=cs: nc.scalar.activation(
                                    out=wSP[:, zb, cs], in_=wE[:, zb, cs], func=AF.Ln, bias=1.0))
                            else:
                                fb = it % 2
                                K.op(act, [R("wE", zb)], [R("wSPf", fb)], lambda zb=zb, cs=cs, fb=fb: nc.scalar.activation(
                                    out=wSPf[:, fb, cs], in_=wE[:, zb, cs], func=AF.Ln, bias=1.0))
                                K.op(pool, [R("wSPf", fb), r_msk], [R("wSP", zb)], lambda zb=zb, cs=cs, fb=fb, rel=rel: nc.gpsimd.tensor_tensor(
                                    out=wSP[:, zb, cs], in0=wSPf[:, fb, cs], in1=msk[:, 1, rel, cs], op=ALU.mult))
                        if 1 <= it <= n:
                            kt, col0, rel, r, j = keys[it - 1]
                            zb = (it - 1) % 3
                            cb = 3 + (it - 1) % 2
                            wb = (it - 1) % 2
                            cs = slice(col0, 512)
                            K.op(pe, [R("wSP", zb), r_const], [RPS[zb]], lambda zb=zb, cs=cs: mm(
                                PS[zb][:, cs], uneg, wSP[:, zb, cs], False, True))
                            K.op(pe, [R("wSP", zb), r_const], [RPS[cb]], lambda zb=zb, cb=cb, cs=cs: mm(
                                PS[cb][:, cs], negones[:], wSP[:, zb, cs], True, True))
                            K.op(dve, [RPS[zb], r_carry], [R("wTMP", zb)], lambda zb=zb, cs=cs: nc.vector.tensor_tensor(
                                out=wTMP[:, zb, cs], in0=PS[zb][:, cs], in1=carry[:, cs], op=ALU.add))
                            if rel is None:
                                K.op(act, [R("wTMP", zb)], [R("wW", wb)], lambda zb=zb, wb=wb, cs=cs: nc.scalar.activation(
                                    out=wW[:, wb, cs], in_=wTMP[:, zb, cs], func=AF.Exp))
                            else:
                                K.op(act, [R("wTMP", zb)], [R("wW", 2 + wb)], lambda zb=zb, wb=wb, cs=cs: nc.scalar.activation(
                                    out=wW[:, 2 + wb, cs], in_=wTMP[:, zb, cs], func=AF.Exp))
                                K.op(pool, [R("wW", 2 + wb), r_msk], [R("wW", wb)], lambda wb=wb, cs=cs, rel=rel: nc.gpsimd.tensor_tensor(
                                    out=wW[:, wb, cs], in0=wW[:, 2 + wb, cs], in1=msk[:, 1, rel, cs], op=ALU.mult))
                            K.op(dve, [RPS[cb], r_carry], [r_carry], lambda cb=cb, cs=cs: nc.vector.tensor_tensor(
                                out=carry[:, cs], in0=PS[cb][:, cs], in1=carry[:, cs], op=ALU.add))
                        if it >= 2:
                            kt, col0, rel, r, j = keys[it - 2]
                            wb = (it - 2) % 2
                            cs = slice(col0, 512)
                            l = r * 16 + j
                            last = (it - 2 == n - 1)
                            K.op(pe, [R("wW", wb), r_v], [RPS[OB]], lambda hb=hb, wb=wb, cs=cs, l=l, last=last: mm(
                                PS[OB][:, cs], vh[:, hb, l, 0:128], wW[:, wb, cs], False, last))
                    os_ = oi[0] % 2
                    oi[0] += 1
                    K.op(act, [RPS[OB]], [R("ost", os_)], lambda os_=os_: nc.scalar.copy(out=ost[:, os_, :], in_=PS[OB][:]))
                    K.dma(sp, [R("ost", os_)], [R("att", h, G)], lambda os_=os_, h=h, G=G: nc.sync.dma_start(
                        out=att_d[h][:, G * 512:(G + 1) * 512], in_=ost[:, os_, :]))
            for hd in range(4 if not MINI else (1 if "d" in MINI else 0)):
                hb = hd % 2
                r_q, r_k, r_v = R("qh", hb), R("kh", hb), R("vh", hb)
                for m in range(2):
                    K.dma(sp, [R("qd_d", 2 * hd + m)], [r_q], lambda hd=hd, hb=hb, m=m: nc.sync.dma_start(
                        out=qh[:, hb, m, :], in_=qd_d[2 * hd + m]))
                    for rr_ in range(2):
                        ch = 2 * hd + m
                        K.dma(sp, [R("kx_all", 2 + ch // 4)], [r_k], lambda ch=ch, hb=hb, m=m, rr_=rr_: nc.sync.dma_start(
                            out=kh[:, hb, m, rr_ * T:(rr_ + 1) * T],
                            in_=kx_all[2 + ch // 4][rr_ * 512 + (ch % 4) * 128:rr_ * 512 + (ch % 4 + 1) * 128, :]))
                for rr_ in range(2):
                    for i in range(4):
                        K.dma(sp, [R("v_all", i)], [r_v], lambda hd=hd, hb=hb, rr_=rr_, i=i: nc.sync.dma_start(
                            out=vh[:, hb, rr_ * 16 + i * 4:rr_ * 16 + i * 4 + 4, :],
                            in_=v_all[i][rr_ * 512:(rr_ + 1) * 512, 1024 + hd * 256:1024 + (hd + 1) * 256].rearrange("(l p) d -> p l d", p=128)))
                for G in range(NG):
                    conv_step(0)
                    keys = key_iter(G)
                    n = len(keys)
                    for bk in range(2, 6):
                        K.op(pe, [r_const, r_q], [RPS[bk]], lambda hb=hb, G=G, bk=bk: mm(
                            PS[bk][:], zeros[:], qh[:, hb, 0, G * 512:(G + 1) * 512], True, False))
                    K.op(dve, [], [R("accd", 0)], lambda: nc.vector.memset(accd[:, 0, :], 0.0))
                    K.op(dve, [], [R("accd", 1)], lambda: nc.vector.memset(accd[:, 1, :], 0.0))
                    for it in range(n + 1):
                        if it < n:
                            kt, col0, rel, r, j = keys[it]
                            cs = slice(col0, 512)
                            kcol = r * T + j * 128
                            for m in range(2):
                                wb = (it % 2) * 2 + m
                                sb_ = m + 6 * (it % 2)
                                K.op(pe, [r_k, r_q], [RPS[sb_]], lambda hb=hb, G=G, m=m, cs=cs, kcol=kcol, col0=col0, sb_=sb_: mm(
                                    PS[sb_][:, cs], kh[:, hb, m, kcol:kcol + 128], qh[:, hb, m, G * 512 + col0:(G + 1) * 512], True, True))
                                if rel is None:
                                    K.op(act, [RPS[sb_]], [R("wW", wb)], lambda sb_=sb_, wb=wb, cs=cs: nc.scalar.activation(
                                        out=wW[:, wb, cs], in_=PS[sb_][:, cs], func=AF.Exp))
                                else:
                                    K.op(act, [RPS[sb_]], [R("wSP", m)], lambda sb_=sb_, m=m, cs=cs: nc.scalar.activation(
                                        out=wSP[:, m, cs], in_=PS[sb_][:, cs], func=AF.Exp))
                                    K.op(pool, [R("wSP", m), r_msk], [R("wW", wb)], lambda m=m, wb=wb, cs=cs, rel=rel: nc.gpsimd.tensor_tensor(
                                        out=wW[:, wb, cs], in0=wSP[:, m, cs], in1=msk[:, 0, rel, cs], op=ALU.mult))
                        if it >= 1:
                            kt, col0, rel, r, j = keys[it - 1]
                            cs = slice(col0, 512)
                            l = r * 16 + j
                            last = (it - 1 == n - 1)
                            for m in range(2):
                                wb = ((it - 1) % 2) * 2 + m
                                for a in range(2):
                                    bk = 2 + 2 * m + a
                                    K.op(pe, [R("wW", wb), r_v], [RPS[bk]], lambda hb=hb, wb=wb, cs=cs, l=l, last=last, a=a, bk=bk: mm(
                                        PS[bk][:, cs], vh[:, hb, l, a * 128:(a + 1) * 128], wW[:, wb, cs], False, last))
                                K.op(dve, [R("wW", wb), R("accd", m)], [R("accd", m)], lambda wb=wb, cs=cs, m=m: nc.vector.tensor_tensor(
                                    out=accd[:, m, cs], in0=accd[:, m, cs], in1=wW[:, wb, cs], op=ALU.add))
                    for m in range(2):
                        K.op(act, [R("accd", m)], [R("hld", m, 0)], lambda m=m: nc.scalar.copy(out=hld[:, m, 0, :], in_=accd[:, m, :]))
                        K.op(dve, [R("accd", m), R("hld", m, 0)], [R("hld", m, 1)], lambda m=m: nc.vector.tensor_tensor(
                            out=hld[:, m, 1, :], in0=accd[:, m, :], in1=hld[:, m, 0, :], op=ALU.subtract))
                        K.op(pe, [R("hld", m, 0), R("hld", m, 1), r_const], [RPS[6 + m]], lambda m=m: (
                            mm(PS[6 + m][:], ones[:], hld[:, m, 0, :], True, False),
                            mm(PS[6 + m][:], ones[:], hld[:, m, 1, :], False, True))[1])
                    r_fin = [R("fin", i) for i in range(8)]
                    K.op(dve, [RPS[6]], [r_fin[0]], lambda: nc.vector.reciprocal(out=fin[:, 0, :], in_=PS[6][:]))
                    K.op(dve, [RPS[7]], [r_fin[1]], lambda: nc.vector.reciprocal(out=fin[:, 1, :], in_=PS[7][:]))
                    K.op(dve, [r_fin[1], r_const], [r_fin[1]], lambda: nc.vector.tensor_scalar(
                        out=fin[:, 1, :], in0=fin[:, 1, :], scalar1=lamc[:, 0:1], scalar2=None, op0=ALU.mult))
                    for a in range(2):
                        K.op(dve, [RPS[2 + a], r_fin[0]], [r_fin[2 + a]], lambda a=a: nc.vector.tensor_tensor(
                            out=fin[:, 2 + a, :], in0=PS[2 + a][:], in1=fin[:, 0, :], op=ALU.mult))
                        K.op(dve, [RPS[4 + a], r_fin[1]], [r_fin[4 + a]], lambda a=a: nc.vector.tensor_tensor(
                            out=fin[:, 4 + a, :], in0=PS[4 + a][:], in1=fin[:, 1, :], op=ALU.mult))
                        K.op(pool, [r_fin[2 + a], r_fin[4 + a]], [r_fin[2 + a]], lambda a=a: nc.gpsimd.tensor_tensor(
                            out=fin[:, 2 + a, :], in0=fin[:, 2 + a, :], in1=fin[:, 4 + a, :], op=ALU.add))
                        K.op(act, [r_fin[2 + a]], [R("sqb", a)], lambda a=a: nc.scalar.activation(
                            out=sqb[:, a, :], in_=fin[:, 2 + a, :], func=AF.Square))
                    K.op(pe, [R("sqb", 0), R("sqb", 1), r_const], [RPS[0]], lambda: (
                        mm(PS[0][:], ones[:], sqb[:, 0, :], True, False), mm(PS[0][:], ones[:], sqb[:, 1, :], False, True))[1])
                    K.op(act, [RPS[0]], [r_fin[6]], lambda: nc.scalar.activation(
                        out=fin[:, 6, :], in_=PS[0][:], func=AF.Sqrt, bias=epsc[:, 0:1], scale=1.0 / 256.0))
                    K.op(dve, [r_fin[6]], [r_fin[6]], lambda: nc.vector.reciprocal(out=fin[:, 6, :], in_=fin[:, 6, :]))
                    for a in range(2):
                        os_ = oi[0] % 2
                        oi[0] += 1
                        K.op(dve, [r_fin[2 + a], r_fin[6], r_const], [R("ost", os_)], lambda a=a, os_=os_: nc.vector.scalar_tensor_tensor(
                            out=ost[:, os_, :], in0=fin[:, 2 + a, :], scalar=subg[:, a:a + 1], in1=fin[:, 6, :],
                            op0=ALU.mult, op1=ALU.mult))
                        K.dma(sp, [R("ost", os_)], [R("att", 8 + 2 * hd + a, G)], lambda os_=os_, hd=hd, a=a, G=G: nc.sync.dma_start(
                            out=att_d[8 + 2 * hd + a][:, G * 512:(G + 1) * 512], in_=ost[:, os_, :]))
            K.barrier()

    def phase_mlp(layer, w_out_d, xin_d, xin_is_input, xmid_d, xout_d, write_xT):
        with ExitStack() as es:
            ATX = es.enter_context(nc.sbuf_tensor("ATX%d" % layer, [128, 16, 512], BF16))
            WB = es.enter_context(nc.sbuf_tensor("WB%d" % layer, [128, 4, 16, 512], BF16))
            XH = es.enter_context(nc.sbuf_tensor("XH%d" % layer, [128, 4, D], F32))
            HT = es.enter_context(nc.sbuf_tensor("HT%d" % layer, [128, 44, 512], BF16))
            GB = es.enter_context(nc.sbuf_tensor("GB%d" % layer, [128, 2, D], F32))
            xb16 = es.enter_context(nc.sbuf_tensor("xb16m%d" % layer, [128, 2, D], BF16))
            sg = es.enter_context(nc.sbuf_tensor("sg%d" % layer, [128, 2, 512], F32))
            st6 = es.enter_context(nc.sbuf_tensor("st6%d" % layer, [128, 24], F32))
            mv = es.enter_context(nc.sbuf_tensor("mv%d" % layer, [128, 4], F32))
            wi = [0]

            def wslot():
                s = wi[0] % 4
                wi[0] += 1
                return s

            xi = [0]
            conv_flush(layer)

            def layer_norm_tile(tt, lnidx, r_xh):
                for q in range(4):
                    K.op(dve, [r_xh], [R("st6")], lambda q=q, tt=tt: nc.vector.bn_stats(
                        out=st6[:, q * 6:(q + 1) * 6], in_=XH[:, tt, q * 512:(q + 1) * 512]))
                K.op(dve, [R("st6")], [R("mv")], lambda: nc.vector.bn_aggr(out=mv[:, 0:2], in_=st6[:]))
                K.op(act, [R("mv")], [R("mv")], lambda: nc.scalar.activation(
                    out=mv[:, 2:3], in_=mv[:, 1:2], func=AF.Sqrt, bias=epsc[:, 0:1], scale=1.0))
                K.op(dve, [R("mv")], [R("mv")], lambda: nc.vector.reciprocal(out=mv[:, 2:3], in_=mv[:, 2:3]))
                K.op(dve, [r_xh, R("mv")], [r_xh], lambda tt=tt: nc.vector.tensor_scalar(
                    out=XH[:, tt, :], in0=XH[:, tt, :], scalar1=mv[:, 0:1], scalar2=mv[:, 2:3],
                    op0=ALU.subtract, op1=ALU.mult))
                K.op(dve, [r_xh, R("GB")], [r_xh], lambda tt=tt: nc.vector.tensor_tensor(
                    out=XH[:, tt, :], in0=XH[:, tt, :], in1=GB[:, 0, :], op=ALU.mult))
                K.op(dve, [r_xh, R("GB")], [r_xh], lambda tt=tt: nc.vector.tensor_tensor(
                    out=XH[:, tt, :], in0=XH[:, tt, :], in1=GB[:, 1, :], op=ALU.add))

            def load_gb(lnidx):
                K.dma(sp, [], [R("GB")], lambda: nc.sync.dma_start(
                    out=GB[:, 0, :], in_=lng[lnidx:lnidx + 1, :].partition_broadcast(128)))
                K.dma(sp, [], [R("GB")], lambda: nc.sync.dma_start(
                    out=GB[:, 1, :], in_=lnb[lnidx:lnidx + 1, :].partition_broadcast(128)))

            for tg in range(NG):
                r_atx = R("ATX")
                r_xh = [R("XH", tt) for tt in range(4)]
                K.dma(sp, [R("att", c, tg) for c in range(16)], [r_atx], lambda tg=tg: nc.sync.dma_start(
                    out=ATX[:], in_=att_d[:, :, tg * 512:(tg + 1) * 512].rearrange("c p t -> p c t")))
                for tt in range(4):
                    row0 = (tg * 4 + tt) * 128
                    K.dma(sp, [R("xres", id(xin_d), tg * 4 + tt)], [r_xh[tt]], lambda tt=tt, row0=row0: nc.sync.dma_start(
                        out=XH[:, tt, :], in_=xin_d[row0:row0 + 128, :]))
                load_gb(2 * layer)
                for nb in range(4):
                    sa = wslot()
                    load_wblock(WB[:, sa], R("WB", sa), wob[layer], nb * 512, 512, deps=conv_res[layer]["wo"])
                    for tt in range(4):
                        pa = next_ps()
                        def fo(sa=sa, tt=tt, pa=pa):
                            ins = None
                            for c in range(16):
                                ins = mm(PS[pa][:], ATX[:, c, tt * 128:(tt + 1) * 128], WB[:, sa, c, :], c == 0, c == 15)
                            return ins
                        K.op(pe, [R("WB", sa), r_atx], [RPS[pa]], fo)
                        K.op(dve, [RPS[pa], r_xh[tt]], [r_xh[tt]], lambda tt=tt, nb=nb, pa=pa: nc.vector.scalar_tensor_tensor(
                            out=XH[:, tt, nb * 512:(nb + 1) * 512], in0=XH[:, tt, nb * 512:(nb + 1) * 512], scalar=ALPHA,
                            in1=PS[pa][:], op0=ALU.mult, op1=ALU.add))
                for tt in range(4):
                    layer_norm_tile(tt, 2 * layer, r_xh[tt])
                    b = xi[0] % 2
                    xi[0] += 1
                    K.op(act, [r_xh[tt]], [R("xb16m", b)], lambda tt=tt, b=b: nc.scalar.copy(out=xb16[:, b, :], in_=XH[:, tt, :]))
                    transpose_rows(xb16[:, b, :], R("xb16m", b), ATX, r_atx, tt * 128)
                load_gb(2 * layer + 1)
                for fb in range(NFB):
                    sg_ = wslot()
                    load_wblock(WB[:, sg_], R("WB", sg_), fgb[layer], fb * 512, 512, deps=conv_res[layer]["fg"])
                    su = wslot()
                    load_wblock(WB[:, su], R("WB", su), fub[layer], fb * 512, 512, deps=conv_res[layer]["fu"])
                    for k in range(4):
                        fc = fb * 4 + k
                        pg = next_ps()
                        pu = next_ps()
                        def fgm(s=sg_, k=k, p=pg):
                            ins = None
                            for c in range(16):
                                ins = mm(PS[p][:], WB[:, s, c, k * 128:(k + 1) * 128], ATX[:, c, :], c == 0, c == 15)
                            return ins
                        def fum(s=su, k=k, p=pu):
                            ins = None
                            for c in range(16):
                                ins = mm(PS[p][:], WB[:, s, c, k * 128:(k + 1) * 128], ATX[:, c, :], c == 0, c == 15)
                            return ins
                        K.op(pe, [R("WB", sg_), r_atx], [RPS[pg]], fgm)
                        K.op(pe, [R("WB", su), r_atx], [RPS[pu]], fum)
                        sb = fc % 2
                        K.op(act, [RPS[pg]], [R("sg", sb)], lambda sb=sb, pg=pg: nc.scalar.activation(
                            out=sg[:, sb, :], in_=PS[pg][:], func=AF.Silu))
                        K.op(dve, [R("sg", sb), RPS[pu]], [R("HT", fc)], lambda sb=sb, pu=pu, fc=fc: nc.vector.tensor_tensor(
                            out=HT[:, fc, :], in0=sg[:, sb, :], in1=PS[pu][:], op=ALU.mult))
                for nb in range(4):
                    banks = [next_ps() for _ in range(4)]
                    for sl in range(4):
                        sa = wslot()
                        load_wblock(WB[:, sa], R("WB", sa), fdb[layer], nb * 512, 512, nchunks=11, row0=sl * 11 * 128,
                                    deps=conv_res[layer]["fd"][2 * sl:2 * sl + 2])
                        for tt in range(4):
                            pa = banks[tt]
                            def fdm(sa=sa, sl=sl, tt=tt, pa=pa):
                                ins = None
                                for c in range(11):
                                    ins = mm(PS[pa][:], HT[:, sl * 11 + c, tt * 128:(tt + 1) * 128], WB[:, sa, c, :],
                                             sl == 0 and c == 0, sl == 3 and c == 10)
                                return ins
                            K.op(pe, [R("WB", sa)] + [R("HT", sl * 11 + c) for c in range(11)], [RPS[pa]], fdm)
                    for tt in range(4):
                        pa = banks[tt]
                        K.op(dve, [RPS[pa], r_xh[tt]], [r_xh[tt]], lambda tt=tt, nb=nb, pa=pa: nc.vector.scalar_tensor_tensor(
                            out=XH[:, tt, nb * 512:(nb + 1) * 512], in0=XH[:, tt, nb * 512:(nb + 1) * 512], scalar=ALPHA,
                            in1=PS[pa][:], op0=ALU.mult, op1=ALU.add))
                for tt in range(4):
                    layer_norm_tile(tt, 2 * layer + 1, r_xh[tt])
                    row0 = (tg * 4 + tt) * 128
                    K.dma(sp, [r_xh[tt]], [R("xres", id(xout_d), tg * 4 + tt)], lambda tt=tt, row0=row0: nc.sync.dma_start(
                        out=xout_d[row0:row0 + 128, :], in_=XH[:, tt, :]))
                    if write_xT:
                        b = xi[0] % 2
                        xi[0] += 1
                        K.op(act, [r_xh[tt]], [R("xb16m", b)], lambda tt=tt, b=b: nc.scalar.copy(out=xb16[:, b, :], in_=XH[:, tt, :]))
                        transpose_rows(xb16[:, b, :], R("xb16m", b), ATX, r_atx, tt * 128)
                if write_xT:
                    K.dma(sp, [r_atx], [R("xT_d", tg)], lambda tg=tg: nc.sync.dma_start(
                        out=xT_d[:, :, tg * 512:(tg + 1) * 512].rearrange("c p t -> p c t"), in_=ATX[:]))
            K.barrier()

    def phase_p4():
        with ExitStack() as es:
            XT = es.enter_context(nc.sbuf_tensor("XT4", [128, 16, T], BF16))
            win = es.enter_context(nc.sbuf_tensor("win", [128, 16, 1088], BF16))
            winr = es.enter_context(nc.sbuf_tensor("winr", [128, 16, 64], BF16))
            tab4 = es.enter_context(nc.sbuf_tensor("tab4", [64, 2, T], F32))
            mgb = es.enter_context(nc.sbuf_tensor("mgb", [128, 1024], F32))
            cT = es.enter_context(nc.sbuf_tensor("cT", [128, 8, T], BF16))
            kpe = es.enter_context(nc.sbuf_tensor("kpe", [64, T], BF16))
            cn = es.enter_context(nc.sbuf_tensor("cn", [128, 2, 1024], BF16))
            junk = es.enter_context(nc.sbuf_tensor("junk", [128, 512], F32))
            ss = es.enter_context(nc.sbuf_tensor("ss", [128, 4], F32))
            t4 = es.enter_context(nc.sbuf_tensor("t4", [64, 2, 512], F32))
            r_XT = [R("XT4", tg) for tg in range(NG)]
            for tg in range(NG):
                K.dma(sp, [R("xT_d", tg)], [r_XT[tg]], lambda tg=tg: nc.sync.dma_start(
                    out=XT[:, :, tg * 512:(tg + 1) * 512], in_=xT_d[:, :, tg * 512:(tg + 1) * 512].rearrange("c p t -> p c t")))
            for q in range(2):
                K.dma(pool, [], [R("win")], lambda q=q: nc.gpsimd.dma_start(
                    out=win[:, :, q * 544:(q + 1) * 544],
                    in_=mwin[:, q * 544:(q + 1) * 544].rearrange("(c p) n -> p c n", p=128)))
            K.dma(pool, [], [R("winr")], lambda: nc.gpsimd.dma_start(
                out=winr[:], in_=mwinr.rearrange("(c p) n -> p c n", p=128)))
            K.dma(sp, [R("ropetab")], [R("tab4")], lambda: nc.sync.dma_start(out=tab4[:], in_=ropetab[0:64, 2:4, :]))
            K.dma(sp, [], [R("mgb")], lambda: nc.sync.dma_start(out=mgb[:], in_=mg_d.partition_broadcast(128)))
            r_cT = R("cT")
            for j in range(NT):
                b = j % 2
                for half in range(2):
                    pa = next_ps()
                    def fc_(half=half, j=j, pa=pa):
                        ins = None
                        for c in range(16):
                            ins = mm(PS[pa][:], XT[:, c, j * 128:(j + 1) * 128], win[:, c, half * 512:(half + 1) * 512], c == 0, c == 15)
                        return ins
                    K.op(pe, [R("win"), r_XT[j // 4]], [RPS[pa]], fc_)
                    K.op(dve, [RPS[pa]], [R("ss")], lambda pa=pa: nc.vector.bn_stats(out=junk[:, 0:6], in_=PS[pa][:]))
                    K.op(dve, [R("ss")], [R("ss")], lambda: nc.vector.bn_aggr(out=junk[:, 8:10], in_=junk[:, 0:6]))
                    K.op(dve, [R("ss")], [R("ss")], lambda: nc.vector.scalar_tensor_tensor(
                        out=junk[:, 10:11], in0=junk[:, 8:9], scalar=junk[:, 8:9], in1=junk[:, 9:10], op0=ALU.mult, op1=ALU.add))
                    K.op(act, [R("ss")], [R("ss")], lambda half=half: nc.scalar.activation(
                        out=ss[:, 2 + half:3 + half], in_=junk[:, 10:11], func=AF.Sqrt, bias=epsc[:, 1:2], scale=1.0))
                    K.op(dve, [R("ss")], [R("ss")], lambda half=half: nc.vector.reciprocal(
                        out=ss[:, 2 + half:3 + half], in_=ss[:, 2 + half:3 + half]))
                    K.op(dve, [RPS[pa], R("ss"), R("mgb")], [R("cn", b)], lambda pa=pa, half=half, b=b: nc.vector.scalar_tensor_tensor(
                        out=cn[:, b, half * 512:(half + 1) * 512], in0=PS[pa][:], scalar=ss[:, 2 + half:3 + half],
                        in1=mgb[:, half * 512:(half + 1) * 512], op0=ALU.mult, op1=ALU.mult))
                for c4 in range(2):
                    pi = next_ps()
                    psb = PS[pi][:].bitcast(BF16)
                    def ft(c4=c4, psb=psb, b=b):
                        ins = None
                        for k in range(4):
                            c = c4 * 4 + k
                            ins = nc.tensor.transpose(psb[:, k * 128:(k + 1) * 128], cn[:, b, c * 128:(c + 1) * 128], ident)
                        return ins
                    K.op(pe, [R("cn", b), r_const], [RPS[pi]], ft)
                    K.op(dve, [RPS[pi]], [r_cT], lambda c4=c4, psb=psb, j=j: nc.vector.tensor_copy(
                        out=cT[:, c4 * 4:(c4 + 1) * 4, j * 128:(j + 1) * 128],
                        in_=psb[:, 0:512].rearrange("p (k n) -> p k n", k=4)))
            for tg in range(NG):
                pa = next_ps()
                pb = next_ps()
                def fa(tg=tg, pa=pa):
                    ins = None
                    for c in range(16):
                        ins = mm(PS[pa][0:64, :], win[:, c, 1024:1088], XT[:, c, tg * 512:(tg + 1) * 512], c == 0, c == 15)
                    return ins
                def fb_(tg=tg, pb=pb):
                    ins = None
                    for c in range(16):
                        ins = mm(PS[pb][0:64, :], winr[:, c, :], XT[:, c, tg * 512:(tg + 1) * 512], c == 0, c == 15)
                    return ins
                K.op(pe, [R("win"), r_XT[tg]], [RPS[pa]], fa)
                K.op(pe, [R("winr"), r_XT[tg]], [RPS[pb]], fb_)
                K.op(dve, [RPS[pa], R("tab4")], [R("t4", 0)], lambda tg=tg, pa=pa: nc.vector.tensor_tensor(
                    out=t4[:, 0, :], in0=PS[pa][0:64, :], in1=tab4[:, 0, tg * 512:(tg + 1) * 512], op=ALU.mult))
                K.op(dve, [RPS[pb], R("tab4")], [R("t4", 1)], lambda tg=tg, pb=pb: nc.vector.tensor_tensor(
                    out=t4[:, 1, :], in0=PS[pb][0:64, :], in1=tab4[:, 1, tg * 512:(tg + 1) * 512], op=ALU.mult))
                K.op(dve, [R("t4", 0), R("t4", 1)], [R("kpe")], lambda tg=tg: nc.vector.tensor_tensor(
                    out=kpe[:, tg * 512:(tg + 1) * 512], in0=t4[:, 0, :], in1=t4[:, 1, :], op=ALU.add))
            K.dma(sp, [r_cT], [R("cq_d")], lambda: nc.sync.dma_start(
                out=cq_d.rearrange("c p t -> p c t"), in_=cT[:, 0:4, :]))
            K.dma(sp, [r_cT], [R("mx_own", 0)], lambda: nc.sync.dma_start(
                out=mx_own[0].rearrange("(c p) t -> p c t", p=128), in_=cT[:, 4:8, :]))
            K.dma(sp, [R("kpe")], [R("mx_own", 1)], lambda: nc.sync.dma_start(out=mx_own[1], in_=kpe[:]))
            for i in range(2):
                K.op(pool, [R("mx_own", i)], [R("mx_all", i)], lambda i=i: nc.gpsimd.collective_compute(
                    "AllGather", ALU.bypass, replica_groups=groups, ins=[mx_own[i].opt()], outs=[mx_all[i].opt()]))
            K.barrier()

    def phase_p5():
        with ExitStack() as es:
            msk = es.enter_context(nc.sbuf_tensor("msk5", [128, 8, 512], BF16))
            ckv = es.enter_context(nc.sbuf_tensor("ckv", [128, 4, 2 * T], BF16))
            kpa = es.enter_context(nc.sbuf_tensor("kpa", [64, 2 * T], BF16))
            cqs = es.enter_context(nc.sbuf_tensor("cqs", [128, 4, T], BF16))
            tab5 = es.enter_context(nc.sbuf_tensor("tab5", [64, 2, T], F32))
            wk = es.enter_context(nc.sbuf_tensor("wk", [128, 4, 512], BF16))
            wv = es.enter_context(nc.sbuf_tensor("wv", [128, 4, 512], BF16))
            wqn = es.enter_context(nc.sbuf_tensor("wqn", [128, 4, 512], BF16))
            wqp = es.enter_context(nc.sbuf_tensor("wqp", [128, 4, 256], BF16))
            wqpr = es.enter_context(nc.sbuf_tensor("wqpr", [128, 4, 256], BF16))
            knT = es.enter_context(nc.sbuf_tensor("knT", [128, 4, 2 * T], BF16))
            V5 = es.enter_context(nc.sbuf_tensor("V5", [128, 32, 512], BF16))
            qn = es.enter_context(nc.sbuf_tensor("qn", [128, 2, T], BF16))
            qp = es.enter_context(nc.sbuf_tensor("qp", [64, 2, T], BF16))
            t5 = es.enter_context(nc.sbuf_tensor("t5", [64, 2, 512], F32))
            pp = es.enter_context(nc.sbuf_tensor("pp", [128, 4, 512], BF16))
            pm = es.enter_context(nc.sbuf_tensor("pm", [128, 2, 512], BF16))
            rr = es.enter_context(nc.sbuf_tensor("rr", [128, 512], F32))
            ost = es.enter_context(nc.sbuf_tensor("ost5", [128, 2, 512], BF16))
            acc5 = es.enter_context(nc.sbuf_tensor("acc5", [128, 2, 512], F32))
            hl5 = es.enter_context(nc.sbuf_tensor("hl5", [128, 2, 512], BF16))
            K.dma(sp, [], [R("msk5")], lambda: nc.sync.dma_start(out=msk[:], in_=masks_d[:, 0]))
            for rr_ in range(2):
                K.dma(sp, [R("mx_all", 0)], [R("ckv")], lambda rr_=rr_: nc.sync.dma_start(
                    out=ckv[:, :, rr_ * T:(rr_ + 1) * T],
                    in_=mx_all[0][rr_ * 512:(rr_ + 1) * 512, :].rearrange("(c p) t -> p c t", p=128)))
                K.dma(sp, [R("mx_all", 1)], [R("kpa")], lambda rr_=rr_: nc.sync.dma_start(
                    out=kpa[:, rr_ * T:(rr_ + 1) * T], in_=mx_all[1][rr_ * 64:(rr_ + 1) * 64, :]))
            K.dma(sp, [R("cq_d")], [R("cqs")], lambda: nc.sync.dma_start(out=cqs[:], in_=cq_d.rearrange("c p t -> p c t")))
            K.dma(sp, [R("ropetab")], [R("tab5")], lambda: nc.sync.dma_start(out=tab5[:], in_=ropetab[0:64, 2:4, :]))
            oi = [0]
            ci = [0]

            for hg in range(4):
                for (dst, rn, src, c0, nc_) in ((wk, "wk", mwkv, hg * 512, 512), (wv, "wv", mwkv, 2048 + hg * 512, 512),
                                                (wqn, "wqn", mwq, hg * 512, 512), (wqp, "wqp", mwq, 2048 + hg * 256, 256),
                                                (wqpr, "wqpr", mwqr, hg * 256, 256)):
                    load_wblock(dst, R(rn), src, c0, nc_, nchunks=4)
                for l in range(32):
                    pa = next_ps()
                    def fv(l=l, pa=pa):
                        ins = None
                        for c in range(4):
                            ins = mm(PS[pa][:], ckv[:, c, l * 128:(l + 1) * 128], wv[:, c, :], c == 0, c == 3)
                        return ins
                    K.op(pe, [R("ckv"), R("wv")], [RPS[pa]], fv)
                    if l % 2 == 0:
                        K.op(act, [RPS[pa]], [R("V5")], lambda l=l, pa=pa: nc.scalar.copy(out=V5[:, l, :], in_=PS[pa][:]))
                    else:
                        K.op(dve, [RPS[pa]], [R("V5")], lambda l=l, pa=pa: nc.vector.tensor_copy(out=V5[:, l, :], in_=PS[pa][:]))
                for hh in range(4):
                    for kg in range(8):
                        pa = next_ps()
                        def fk(hh=hh, kg=kg, pa=pa):
                            ins = None
                            for c in range(4):
                                ins = mm(PS[pa][:], wk[:, c, hh * 128:(hh + 1) * 128], ckv[:, c, kg * 512:(kg + 1) * 512], c == 0, c == 3)
                            return ins
                        K.op(pe, [R("ckv"), R("wk")], [RPS[pa]], fk)
                        if kg % 2 == 0:
                            K.op(act, [RPS[pa]], [R("knT")], lambda hh=hh, kg=kg, pa=pa: nc.scalar.copy(
                                out=knT[:, hh, kg * 512:(kg + 1) * 512], in_=PS[pa][:]))
                        else:
                            K.op(dve, [RPS[pa]], [R("knT")], lambda hh=hh, kg=kg, pa=pa: nc.vector.tensor_copy(
                                out=knT[:, hh, kg * 512:(kg + 1) * 512], in_=PS[pa][:]))
                for hh in range(4):
                    h = hg * 4 + hh
                    qb = h % 2
                    r_qn, r_qp = R("qn", qb), R("qp", qb)
                    for tg in range(NG):
                        pa, pb, pc = next_ps(), next_ps(), next_ps()
                        def fqn(hh=hh, tg=tg, pa=pa):
                            ins = None
                            for c in range(4):
                                ins = mm(PS[pa][:], wqn[:, c, hh * 128:(hh + 1) * 128], cqs[:, c, tg * 512:(tg + 1) * 512], c == 0, c == 3)
                            return ins
                        def fqa(hh=hh, tg=tg, pb=pb):
                            ins = None
                            for c in range(4):
                                ins = mm(PS[pb][0:64, :], wqp[:, c, hh * 64:(hh + 1) * 64], cqs[:, c, tg * 512:(tg + 1) * 512], c == 0, c == 3)
                            return ins
                        def fqb(hh=hh, tg=tg, pc=pc):
                            ins = None
                            for c in range(4):
                                ins = mm(PS[pc][0:64, :], wqpr[:, c, hh * 64:(hh + 1) * 64], cqs[:, c, tg * 512:(tg + 1) * 512], c == 0, c == 3)
                            return ins
                        K.op(pe, [R("cqs"), R("wqn")], [RPS[pa]], fqn)
                        K.op(pe, [R("cqs"), R("wqp")], [RPS[pb]], fqa)
                        K.op(pe, [R("cqs"), R("wqpr")], [RPS[pc]], fqb)
                        K.op(act, [RPS[pa]], [r_qn], lambda qb=qb, tg=tg, pa=pa: nc.scalar.mul(
                            out=qn[:, qb, tg * 512:(tg + 1) * 512], in_=PS[pa][:], mul=SC192))
                        K.op(dve, [RPS[pb], R("tab5")], [R("t5", 0)], lambda tg=tg, pb=pb: nc.vector.tensor_tensor(
                            out=t5[:, 0, :], in0=PS[pb][0:64, :], in1=tab5[:, 0, tg * 512:(tg + 1) * 512], op=ALU.mult))
                        K.op(dve, [RPS[pc], R("tab5")], [R("t5", 1)], lambda tg=tg, pc=pc: nc.vector.tensor_tensor(
                            out=t5[:, 1, :], in0=PS[pc][0:64, :], in1=tab5[:, 1, tg * 512:(tg + 1) * 512], op=ALU.mult))
                        K.op(dve, [R("t5", 0), R("t5", 1)], [R("t5", 0)], lambda: nc.vector.tensor_tensor(
                            out=t5[:, 0, :], in0=t5[:, 0, :], in1=t5[:, 1, :], op=ALU.add))
                        K.op(act, [R("t5", 0)], [r_qp], lambda qb=qb, tg=tg: nc.scalar.mul(
                            out=qp[:, qb, tg * 512:(tg + 1) * 512], in_=t5[:, 0, :], mul=SC192))
                    for G in range(NG):
                        conv_step(1)
                        keys = key_iter(G)
                        n = len(keys)
                        OB, RB = 6, 7
                        K.op(pe, [r_const, r_qn], [RPS[OB]], lambda qb=qb, G=G: mm(
                            PS[OB][:], zeros[:], qn[:, qb, G * 512:(G + 1) * 512], True, False))
                        K.op(dve, [], [R("acc5", 0)], lambda: nc.vector.memset(acc5[:, 0, :], 0.0))
                        for it in range(n + 1):
                            if it < n:
                                kt, col0, rel, r, j = keys[it]
                                cs = slice(col0, 512)
                                kcol = r * T + j * 128
                                sbk = it % 3
                                wb = it % 4
                                def fs(hh=hh, qb=qb, G=G, cs=cs, kcol=kcol, col0=col0, sbk=sbk):
                                    mm(PS[sbk][:, cs], knT[:, hh, kcol:kcol + 128], qn[:, qb, G * 512 + col0:(G + 1) * 512], True, False)
                                    return mm(PS[sbk][:, cs], kpa[:, kcol:kcol + 128], qp[:, qb, G * 512 + col0:(G + 1) * 512], False, True)
                                K.op(pe, [R("knT"), R("kpa"), r_qn, r_qp], [RPS[sbk]], fs)
                                if rel is None:
                                    K.op(act, [RPS[sbk]], [R("pp", wb)], lambda sbk=sbk, wb=wb, cs=cs: nc.scalar.activation(
                                        out=pp[:, wb, cs], in_=PS[sbk][:, cs], func=AF.Exp))
                                else:
                                    mb = it % 2
                                    K.op(act, [RPS[sbk]], [R("pm", mb)], lambda sbk=sbk, mb=mb, cs=cs: nc.scalar.activation(
                                        out=pm[:, mb, cs], in_=PS[sbk][:, cs], func=AF.Exp))
                                    K.op(pool, [R("pm", mb), R("msk5")], [R("pp", wb)], lambda mb=mb, wb=wb, cs=cs, rel=rel: nc.gpsimd.tensor_tensor(
                                        out=pp[:, wb, cs], in0=pm[:, mb, cs], in1=msk[:, rel, cs], op=ALU.mult))
                            if it >= 1:
                                kt, col0, rel, r, j = keys[it - 1]
                                cs = slice(col0, 512)
                                l = r * 16 + j
                                last = (it - 1 == n - 1)
                                wb = (it - 1) % 4
                                K.op(pe, [R("pp", wb), R("V5")], [RPS[OB]], lambda hh=hh, wb=wb, cs=cs, l=l, last=last: mm(
                                    PS[OB][:, cs], V5[:, l, hh * 128:(hh + 1) * 128], pp[:, wb, cs], False, last))
                                K.op(dve, [R("pp", wb), R("acc5", 0)], [R("acc5", 0)], lambda wb=wb, cs=cs: nc.vector.tensor_tensor(
                                    out=acc5[:, 0, cs], in0=acc5[:, 0, cs], in1=pp[:, wb, cs], op=ALU.add))
                        K.op(act, [R("acc5", 0)], [R("hl5", 0)], lambda: nc.scalar.copy(out=hl5[:, 0, :], in_=acc5[:, 0, :]))
                        K.op(dve, [R("acc5", 0), R("hl5", 0)], [R("hl5", 1)], lambda: nc.vector.tensor_tensor(
                            out=hl5[:, 1, :], in0=acc5[:, 0, :], in1=hl5[:, 0, :], op=ALU.subtract))
                        K.op(pe, [R("hl5", 0), R("hl5", 1), r_const], [RPS[RB]], lambda: (
                            mm(PS[RB][:], ones[:], hl5[:, 0, :], True, False), mm(PS[RB][:], ones[:], hl5[:, 1, :], False, True))[1])
                        K.op(dve, [RPS[RB]], [R("rr")], lambda: nc.vector.reciprocal(out=rr[:], in_=PS[RB][:]))
                        os_ = oi[0] % 2
                        oi[0] += 1
                        K.op(dve, [RPS[OB], R("rr")], [R("ost5", os_)], lambda os_=os_: nc.vector.tensor_tensor(
                            out=ost[:, os_, :], in0=PS[OB][:], in1=rr[:], op=ALU.mult))
                        K.dma(sp, [R("ost5", os_)], [R("att", h, G)], lambda os_=os_, h=h, G=G: nc.sync.dma_start(
                            out=att_d[h][:, G * 512:(G + 1) * 512], in_=ost[:, os_, :]))
            K.barrier()

    MINI = os.environ.get("KMINI", "")
    if not MINI:
        phase_p1()
    if debug and not MINI:
        dk = nc.dram_tensor("dbg_kx", [4, 1024, T], BF16, kind="ExternalOutput").ap()
        dv = nc.dram_tensor("dbg_v", [4, 1024, 2048], BF16, kind="ExternalOutput").ap()
        for i in range(4):
            K.dma(sp, [R("kx_all", i)], [R("dbg_kx")], lambda i=i: nc.sync.dma_start(out=dk[i], in_=kx_all[i]))
            K.dma(sp, [R("v_all", i)], [R("dbg_v")], lambda i=i: nc.sync.dma_start(out=dv[i], in_=v_all[i]))
    if stop_after >= 2:
        phase_p2()
    if stop_after >= 3:
        phase_mlp(0, wo0, x_own, True, None, xa_d, True)
    if stop_after >= 4:
        phase_p4()
    if debug and stop_after >= 4:
        dm = nc.dram_tensor("dbg_mx", [1152, T], BF16, kind="ExternalOutput").ap()
        K.dma(sp, [R("mx_all", 0)], [R("dbg_mx")], lambda: nc.sync.dma_start(out=dm[0:1024, :], in_=mx_all[0]))
        K.dma(sp, [R("mx_all", 1)], [R("dbg_mx")], lambda: nc.sync.dma_start(out=dm[1024:1152, :], in_=mx_all[1]))
    if stop_after >= 5:
        phase_p5()
    if stop_after >= 6:
        phase_mlp(1, mwo, xa_d, False, None, y_out, False)
    K.barrier()
    nc._used_inputs = used_inputs
    return nc


def _bf(a):
    return np.ascontiguousarray(a).astype(ml_dtypes.bfloat16)


def make_core_inputs(inputs, b, r, shared):
    x = inputs["x"][b]
    posb = inputs["positions"][b]
    idx = np.concatenate([np.arange(gtile(r, j) * 128, gtile(r, j) * 128 + 128) for j in range(NT)])
    d = dict(shared)
    d["x_own"] = np.ascontiguousarray(x[idx])
    d["pos"] = np.ascontiguousarray(posb[idx].reshape(1, T).astype(np.int32))
    d["masks"] = shared["_masks"][r]
    del d["_masks"]
    return d, idx


def make_shared(inputs):
    f32 = np.float32
    sh = {}
    masks = []
    s_ = np.arange(128)[:, None]
    for r in range(2):
        m = np.zeros((128, 2, 8, 512), f32)
        for rel in range(8):
            for jj in range(4):
                gq = 2 * jj + ((jj & 1) ^ r)
                tq = np.arange(128)[None, :]
                if rel < gq:
                    ns = np.ones((128, 128), f32)
                    st = ns
                elif rel == gq:
                    ns = (s_ <= tq).astype(f32)
                    st = (s_ < tq).astype(f32)
                else:
                    ns = np.zeros((128, 128), f32)
                    st = ns
                m[:, 0, rel, jj * 128:(jj + 1) * 128] = ns
                m[:, 1, rel, jj * 128:(jj + 1) * 128] = st
        masks.append(_bf(m))
    sh["_masks"] = masks
    cst = np.zeros((128, 2, 128), f32)
    cst[:, 0, :] = np.eye(128)
    jj_, ss_ = np.arange(128)[:, None], np.arange(128)[None, :]
    cst[:, 1, :] = -(jj_ >= ss_).astype(f32)
    sh["cstb"] = _bf(cst)
    colc = np.zeros((128, 8), f32)
    inv128 = (f32(10000.0) ** (-np.arange(0, 128, 2, dtype=f32) / f32(128))).astype(f32)
    inv64 = (f32(10000.0) ** (-np.arange(0, 64, 2, dtype=f32) / f32(64))).astype(f32)
    p = np.arange(128)
    colc[:, 0] = inv128[p % 64]
    colc[:, 1] = np.where(p < 64, -1.0, 1.0)
    colc[:64, 2] = inv64[p[:64] % 32]
    colc[:64, 3] = np.where(p[:64] < 32, -1.0, 1.0)
    sh["colc"] = colc
    w0 = np.ascontiguousarray(inputs["sb_diff_w_in"][0])
    sh["w0"] = w0
    def rot(wc, hd):
        n = wc.shape[1] // hd
        w3 = wc.reshape(wc.shape[0], n, hd)
        return np.ascontiguousarray(np.concatenate([w3[:, :, hd // 2:], w3[:, :, :hd // 2]], axis=2).reshape(wc.shape[0], n * hd))
    sh["w0r"] = np.ascontiguousarray(np.concatenate([rot(w0[:, 3072:4096], 128), rot(w0[:, 4096:5120], 128)], axis=1))
    sh["wo0"] = np.ascontiguousarray(inputs["sb_diff_w_out"][0])
    sh["lamv"] = np.ascontiguousarray(np.concatenate([inputs["diff_lambda_q1"][0], inputs["diff_lambda_k1"][0],
                                                      inputs["diff_lambda_q2"][0], inputs["diff_lambda_k2"][0]]).reshape(1, 512).astype(f32))
    sh["subg"] = np.ascontiguousarray(inputs["diff_subln_g"][0].reshape(2, 128).T.astype(f32))
    mwin = np.ascontiguousarray(inputs["mla_w_in"][0])
    sh["mwin"] = mwin
    sh["mwinr"] = rot(mwin[:, 1024:1088], 64)
    sh["mg"] = np.ascontiguousarray(np.concatenate([inputs["mla_q_norm_g"][0], inputs["mla_kv_norm_g"][0]]).reshape(1, 1024).astype(f32))
    wq = inputs["mla_w_q_up"][0].reshape(512, 16, 192)
    wq_n = wq[:, :, :128].reshape(512, 2048)
    wq_p = np.ascontiguousarray(wq[:, :, 128:].reshape(512, 1024))
    sh["mwq"] = np.ascontiguousarray(np.concatenate([wq_n, wq_p], axis=1))
    sh["mwqr"] = rot(wq_p, 64)
    wkv = inputs["mla_w_kv_up"][0].reshape(512, 16, 256)
    sh["mwkv"] = np.ascontiguousarray(np.concatenate([wkv[:, :, :128].reshape(512, 2048), wkv[:, :, 128:].reshape(512, 2048)], axis=1))
    sh["mwo"] = np.ascontiguousarray(inputs["mla_w_out"][0])
    sh["fg"] = np.ascontiguousarray(inputs["ffn_w_gate"])
    sh["fu"] = np.ascontiguousarray(inputs["ffn_w_up"])
    sh["fd"] = np.ascontiguousarray(inputs["ffn_w_down"])
    sh["lng"] = np.ascontiguousarray(inputs["ln_g"].reshape(4, D).astype(f32))
    sh["lnb"] = np.ascontiguousarray(inputs["ln_b"].reshape(4, D).astype(f32))
    return sh


_NC_CACHE = {}


def kernel(**inputs):
    inputs = {k: np.asarray(v) for k, v in inputs.items()}
    stop_after = int(os.environ.get("KSTOP", "99"))
    debug = os.environ.get("KDEBUG", "0") == "1"
    key = (stop_after, debug)
    if key not in _NC_CACHE:
        _NC_CACHE[key] = build(stop_after, debug)
    nc = _NC_CACHE[key]
    shared = make_shared(inputs)
    in_maps, idxs = [], []
    for core in range(8):
        b, r = core // 2, core % 2
        d, idx = make_core_inputs(inputs, b, r, shared)
        d = {k: v for k, v in d.items() if k in nc._used_inputs}
        in_maps.append(d)
        idxs.append(idx)
    res = run_bass_kernel_spmd(nc, in_maps, core_ids=list(range(8)))
    out = np.zeros((4, S, D), np.float32)
    for core in range(8):
        b = core // 2
        out[b, idxs[core], :] = np.asarray(res.results[core]["y"])
    if debug:
        kernel.last = res
    return out
```

```python
import os
import math
from contextlib import ExitStack
import numpy as np
import ml_dtypes
import concourse.bass as bass
import concourse.mybir as mybir
from concourse.bass_utils import run_bass_kernel_spmd

F32 = mybir.dt.float32
BF16 = mybir.dt.bfloat16
I32 = mybir.dt.int32
AF = mybir.ActivationFunctionType
ALU = mybir.AluOpType
AX = mybir.AxisListType

D = 2048
T = 2048
NT = 16
NG = 4
S = 4096
FF = 5632
NFB = 11
ALPHA = 4.0 ** 0.25
LN_EPS = 1e-5
RMS_EPS = 1e-6
SC128 = 128.0 ** -0.5
SC192 = 192.0 ** -0.5
PI = math.pi


def gtile(r, j):
    return 2 * j + ((j & 1) ^ r)


def loc_of(g):
    j = g // 2
    r = (g & 1) ^ (j & 1)
    return r, j


class Res:
    __slots__ = ("name", "w", "r")

    def __init__(self, name=""):
        self.name = name
        self.w = None
        self.r = []


class Eng:
    def __init__(self, name, eng):
        self.name = name
        self.eng = eng
        self.seen = {}
        self.sem = None
        self.cnt = 0


class KB:
    LIMIT = int(os.environ.get('KLIMIT', '1000'))
    NRING = 8

    def __init__(self, nc):
        self.nc = nc
        self.nsem = 0
        self.allsems = []
        self.pe = Eng("pe", nc.tensor)
        self.act = Eng("act", nc.scalar)
        self.dve = Eng("dve", nc.vector)
        self.pool = Eng("pool", nc.gpsimd)
        self.sp = Eng("sp", nc.sync)
        self.engs = [self.pe, self.act, self.dve, self.pool, self.sp]
        for e in self.engs:
            e.sem = self._newsem(e.name)
        self.dq = {}
        for e in (self.sp, self.pool):
            self.dq[e.name] = {"ring": [[self._newsem(e.name + "d"), 0] for _ in range(self.NRING)], "i": 0}
        self.res = {}

    def R(self, *key):
        r = self.res.get(key)
        if r is None:
            r = Res(str(key))
            self.res[key] = r
        return r

    def _newsem(self, name):
        self.nsem += 1
        h = self.nc.alloc_semaphore(name="s%s%d" % (name, self.nsem))
        self.allsems.append(h)
        return h

    def _wait(self, e, deps):
        best = {}
        for (sem, cnt) in deps:
            k = id(sem)
            if k not in best or best[k][1] < cnt:
                best[k] = (sem, cnt)
        for k, (sem, cnt) in best.items():
            if e.seen.get(k, 0) >= cnt:
                continue
            e.eng.wait_ge(sem, cnt)
            e.seen[k] = cnt

    @staticmethod
    def _deps(reads, writes):
        deps = []
        for r in reads:
            if r.w is not None:
                deps.append(r.w)
        for w in writes:
            if w.w is not None:
                deps.append(w.w)
            deps.extend(w.r)
        return deps

    @staticmethod
    def _commit(tok, reads, writes):
        for r in reads:
            r.r.append(tok)
            if len(r.r) > 48:
                best = {}
                for (s, c) in r.r:
                    if id(s) not in best or best[id(s)][1] < c:
                        best[id(s)] = (s, c)
                r.r = list(best.values())
        for w in writes:
            w.w = tok
            w.r = []

    def op(self, e, reads, writes, fn):
        self._wait(e, self._deps(reads, writes))
        inst = fn()
        if e.cnt >= self.LIMIT:
            e.sem = self._newsem(e.name)
            e.cnt = 0
        e.cnt += 1
        inst.then_inc(e.sem, 1)
        tok = (e.sem, e.cnt)
        self._commit(tok, reads, writes)
        return tok

    def dma(self, e, reads, writes, fn):
        q = self.dq[e.name]
        slot = q["ring"][q["i"] % self.NRING]
        q["i"] += 1
        deps = self._deps(reads, writes)
        if slot[1] > 0:
            deps.append((slot[0], slot[1]))
        self._wait(e, deps)
        inst = fn()
        if slot[1] >= self.LIMIT:
            slot[0] = self._newsem(e.name + "d")
            slot[1] = 0
        slot[1] += 16
        inst.then_inc(slot[0], 16)
        tok = (slot[0], slot[1])
        self._commit(tok, reads, writes)
        return tok

    def barrier(self):
        deps = []
        for e in self.engs:
            if e.cnt > 0:
                deps.append((e.sem, e.cnt))
        for q in self.dq.values():
            for slot in q["ring"]:
                if slot[1] > 0:
                    deps.append((slot[0], slot[1]))
        for e in self.engs:
            self._wait(e, deps)

    def finish(self, res_list):
        deps = []
        for r in res_list:
            if r.w is not None:
                deps.append(r.w)
        self._wait(self.sp, deps)


def build(stop_after=99, debug=False):
    nc = bass.Bass("TRN2", target_bir_lowering=False)
    K = KB(nc)
    R = K.R

    need = {"x_own": 1, "pos": 1, "masks": 2, "cstb": 1, "colc": 1, "w0": 1, "w0r": 1, "wo0": 3, "lamv": 1, "subg": 1,
            "mwin": 4, "mwinr": 4, "mg": 4, "mwq": 5, "mwqr": 5, "mwkv": 5, "mwo": 6, "fg": 3, "fu": 3, "fd": 3,
            "lng": 3, "lnb": 3}
    used_inputs = []

    def din(name, shape, dt=F32):
        if need[name] > stop_after:
            return None
        used_inputs.append(name)
        return nc.dram_tensor(name, list(shape), dt, kind="ExternalInput").ap()

    def dscr(name, shape, dt=BF16):
        kind = "ExternalOutput" if (debug and name in DEBUG_OUT) else "Internal"
        return nc.dram_tensor(name, list(shape), dt, kind=kind).ap()

    DEBUG_OUT = ("qa", "qd", "att", "xa", "xb", "cq", "ropetab", "xT")

    x_own = din("x_own", [T, D])
    pos = din("pos", [1, T], I32)
    masks_d = din("masks", [128, 2, 8, 512], BF16)
    cstb_d = din("cstb", [128, 2, 128], BF16)
    colc_d = din("colc", [128, 8])
    w0 = din("w0", [D, 6144])
    w0r = din("w0r", [D, 2048])
    wo0 = din("wo0", [D, D])
    lamv_d = din("lamv", [1, 4 * 128])
    subg_d = din("subg", [128, 2])
    mwin = din("mwin", [D, 1088])
    mwinr = din("mwinr", [D, 64])
    mg_d = din("mg", [1, 1024])
    mwq = din("mwq", [512, 3072])
    mwqr = din("mwqr", [512, 1024])
    mwkv = din("mwkv", [512, 4096])
    mwo = din("mwo", [D, D])
    fg = din("fg", [2, D, FF])
    fu = din("fu", [2, D, FF])
    fd = din("fd", [2, FF, D])
    lng = din("lng", [4, D])
    lnb = din("lnb", [4, D])
    y_out = nc.dram_tensor("y", [T, D], F32, kind="ExternalOutput").ap()

    ropetab = dscr("ropetab", [128, 4, T], F32)
    qa_d = dscr("qa", [8, 128, T])
    qd_d = dscr("qd", [8, 128, T])
    kx_own = [dscr("kx_own%d" % i, [512, T]) for i in range(4)]
    kx_all = [dscr("kx_all%d" % i, [1024, T]) for i in range(4)]
    v_own = [dscr("v_own%d" % i, [512, 2048]) for i in range(4)]
    v_all = [dscr("v_all%d" % i, [1024, 2048]) for i in range(4)]
    att_d = dscr("att", [16, 128, T])
    xa_d = dscr("xa", [T, D], F32)
    xb_d = dscr("xb", [T, D], F32)
    xT_d = dscr("xT", [16, 128, T])
    cq_d = dscr("cq", [4, 128, T])
    mx_own = [dscr("mx_own0", [512, T]), dscr("mx_own1", [64, T])]
    mx_all = [dscr("mx_all0", [1024, T]), dscr("mx_all1", [128, T])]

    groups = [[0, 1], [2, 3], [4, 5], [6, 7]]
    wob = [dscr("wob%d" % l, [D, D]) for l in range(2)]
    fgb = [dscr("fgb%d" % l, [D, FF]) for l in range(2)]
    fub = [dscr("fub%d" % l, [D, FF]) for l in range(2)]
    fdb = [dscr("fdb%d" % l, [FF, D]) for l in range(2)]
    conv_units = [[], []]
    conv_res = [{"wo": [], "fg": [], "fu": [], "fd": []} for _ in range(2)]

    cstb = nc.alloc_sbuf_tensor("cstb_s", [128, 2, 128], BF16)
    ones = nc.alloc_sbuf_tensor("ones", [128, 128], BF16)
    negones = nc.alloc_sbuf_tensor("negones", [128, 128], BF16)
    zeros = nc.alloc_sbuf_tensor("zeros", [128, 128], BF16)
    colc = nc.alloc_sbuf_tensor("colc_s", [128, 8], F32)
    lamc = nc.alloc_sbuf_tensor("lamc", [128, 4], F32)
    subg = nc.alloc_sbuf_tensor("subg_s", [128, 2], F32)
    epsc = nc.alloc_sbuf_tensor("epsc", [128, 2], F32)
    ident = cstb[:, 0, :]
    uneg = cstb[:, 1, :]
    PS = [nc.alloc_psum_tensor("ps%d" % i, [128, 512], F32) for i in range(8)]
    RPS = [R("ps", i) for i in range(8)]
    r_const = R("const")

    pe, act, dve, pool, sp = K.pe, K.act, K.dve, K.pool, K.sp

    def mm(out, lhsT, rhs, start, stop):
        return nc.tensor.matmul(out, lhsT, rhs, start=start, stop=stop, skip_group_check=True)

    K.dma(sp, [], [r_const], lambda: nc.sync.dma_start(out=cstb[:], in_=cstb_d))
    K.dma(sp, [], [r_const], lambda: nc.sync.dma_start(out=colc[:], in_=colc_d))
    K.dma(sp, [], [r_const], lambda: nc.sync.dma_start(out=subg[:], in_=subg_d))
    K.op(pool, [], [r_const], lambda: nc.gpsimd.memset(ones[:], 1.0))
    K.op(pool, [], [r_const], lambda: nc.gpsimd.memset(negones[:], -1.0))
    K.op(pool, [], [r_const], lambda: nc.gpsimd.memset(zeros[:], 0.0))
    K.op(pool, [], [r_const], lambda: nc.gpsimd.memset(epsc[:, 0:1], LN_EPS))
    K.op(pool, [], [r_const], lambda: nc.gpsimd.memset(epsc[:, 1:2], RMS_EPS))
    K.op(dve, [r_const], [r_const], lambda: nc.vector.tensor_scalar(
        out=subg[:], in0=subg[:], scalar1=0.8, scalar2=None, op0=ALU.mult))

    with ExitStack() as es:
        posi = es.enter_context(nc.sbuf_tensor("p0a", [128, T], I32))
        posf = es.enter_context(nc.sbuf_tensor("p0b", [128, T], F32))
        ang = es.enter_context(nc.sbuf_tensor("p0c", [128, T], F32))
        tab = es.enter_context(nc.sbuf_tensor("p0d", [128, 4, T], F32))
        lv = es.enter_context(nc.sbuf_tensor("p0e", [128, 4, 128], F32))
        lprod = es.enter_context(nc.sbuf_tensor("p0f", [128, 128], F32))
        kf = es.enter_context(nc.sbuf_tensor("p0g", [128, T], F32))
        r_posi, r_posf, r_ang, r_tab, r_lv, r_lp = R("posi"), R("posf"), R("ang"), R("tab"), R("lv"), R("lprod")
        K.dma(sp, [], [r_posi], lambda: nc.sync.dma_start(out=posi[:], in_=pos.partition_broadcast(128)))
        K.op(dve, [r_posi], [r_posf], lambda: nc.vector.tensor_copy(out=posf[:], in_=posi[:]))
        C1 = 6.28125
        C2 = 2.0 * PI - 6.28125
        for ti, (fcol, scol, off) in enumerate([(0, None, 0.5 * PI), (0, 1, 0.0), (2, None, 0.5 * PI), (2, 3, 0.0)]):
            K.op(dve, [r_posf, r_const], [r_ang], lambda fcol=fcol, off=off: nc.vector.tensor_scalar(
                out=ang[:], in0=posf[:], scalar1=colc[:, fcol:fcol + 1], scalar2=off, op0=ALU.mult, op1=ALU.add))
            K.op(dve, [r_ang], [r_posi], lambda: nc.vector.tensor_scalar(
                out=posi[:], in0=ang[:], scalar1=1.0 / (2.0 * PI), scalar2=None, op0=ALU.mult))
            K.op(dve, [r_posi], [R("kf")], lambda: nc.vector.tensor_copy(out=kf[:], in_=posi[:]))
            K.op(dve, [R("kf"), r_ang], [r_ang], lambda: nc.vector.scalar_tensor_tensor(
                out=ang[:], in0=kf[:], scalar=-C1, in1=ang[:], op0=ALU.mult, op1=ALU.add))
            K.op(dve, [R("kf"), r_ang], [r_ang], lambda: nc.vector.scalar_tensor_tensor(
                out=ang[:], in0=kf[:], scalar=-C2, in1=ang[:], op0=ALU.mult, op1=ALU.add))
            K.op(dve, [r_ang], [r_ang], lambda: nc.vector.tensor_scalar(
                out=ang[:], in0=ang[:], scalar1=-PI, scalar2=PI, op0=ALU.max, op1=ALU.min))
            K.op(act, [r_ang], [r_tab], lambda ti=ti: nc.scalar.activation(
                out=tab[:, ti, :], in_=ang[:], func=AF.Sin))
            if scol is not None:
                K.op(dve, [r_tab, r_const], [r_tab], lambda ti=ti, scol=scol: nc.vector.tensor_scalar(
                    out=tab[:, ti, :], in0=tab[:, ti, :], scalar1=colc[:, scol:scol + 1], scalar2=None, op0=ALU.mult))
        r_ropetab = R("ropetab")
        K.dma(sp, [r_tab], [r_ropetab], lambda: nc.sync.dma_start(out=ropetab, in_=tab[:]))
        K.dma(sp, [], [r_lv], lambda: nc.sync.dma_start(
            out=lv[:].rearrange("p a b -> p (a b)"), in_=lamv_d.partition_broadcast(128)))
        for i in range(2):
            K.op(dve, [r_lv], [r_lp], lambda i=i: nc.vector.tensor_tensor(
                out=lprod[:], in0=lv[:, 2 * i, :], in1=lv[:, 2 * i + 1, :], op=ALU.mult))
            K.op(dve, [r_lp], [r_const], lambda i=i: nc.vector.reduce_sum(
                out=lamc[:, 1 + i:2 + i], in_=lprod[:], axis=AX.X))
        K.op(act, [r_const], [r_const], lambda: nc.scalar.activation(out=lamc[:, 1:3], in_=lamc[:, 1:3], func=AF.Exp))
        K.op(dve, [r_const], [r_const], lambda: nc.vector.tensor_tensor(
            out=lamc[:, 0:1], in0=lamc[:, 2:3], in1=lamc[:, 1:2], op=ALU.subtract))
        K.op(dve, [r_const], [r_const], lambda: nc.vector.tensor_scalar(
            out=lamc[:, 0:1], in0=lamc[:, 0:1], scalar1=-0.2, scalar2=None, op0=ALU.add))
        K.barrier()

    def conv_setup(layer, wo_src):
        if wo_src is None or fg is None:
            return
        u = conv_units[layer]
        for i in range(4):
            rr_ = R("cv", layer, "wo", i)
            conv_res[layer]["wo"].append(rr_)
            u.append((wo_src[i * 512:(i + 1) * 512, :], wob[layer][i * 512:(i + 1) * 512, :], rr_))
        for i in range(16):
            for nm, src, dst in (("fg", fg, fgb), ("fu", fu, fub)):
                rr_ = R("cv", layer, nm, i)
                conv_res[layer][nm].append(rr_)
                u.append((src[layer][i * 128:(i + 1) * 128, :], dst[layer][i * 128:(i + 1) * 128, :], rr_))
        for i in range(8):
            rr_ = R("cv", layer, "fd", i)
            conv_res[layer]["fd"].append(rr_)
            u.append((fd[layer][i * 704:(i + 1) * 704, :], fdb[layer][i * 704:(i + 1) * 704, :], rr_))

    def conv_step(layer, n=1):
        for _ in range(n):
            if not conv_units[layer]:
                return
            src, dst, rr_ = conv_units[layer].pop(0)
            K.dma(pool, [], [rr_], lambda src=src, dst=dst: nc.gpsimd.dma_start(out=dst, in_=src))

    def conv_flush(layer):
        conv_step(layer, len(conv_units[layer]))

    conv_setup(0, wo0)
    conv_setup(1, mwo)

    psrot = [0]

    def next_ps():
        i = psrot[0] % 8
        psrot[0] += 1
        return i

    def load_wblock(dst, rdst, wsrc, col0, ncols, nchunks=16, row0=0, deps=()):
        src = wsrc[row0:row0 + nchunks * 128, col0:col0 + ncols].rearrange("(c p) n -> p c n", p=128)
        K.dma(pool, list(deps), [rdst], lambda: nc.gpsimd.dma_start(out=dst[:, 0:nchunks, 0:ncols], in_=src))

    def transpose_rows(src_bf, r_src, dstT, r_dst, tcol0):
        for c4 in range(4):
            pi = next_ps()
            psb = PS[pi][:].bitcast(BF16)
            def f(c4=c4, psb=psb):
                ins = None
                for k in range(4):
                    c = c4 * 4 + k
                    ins = nc.tensor.transpose(psb[:, k * 128:(k + 1) * 128], src_bf[:, c * 128:(c + 1) * 128], ident)
                return ins
            K.op(pe, [r_src, r_const], [RPS[pi]], f)
            eng = dve if c4 % 2 == 0 else act
            if eng is dve:
                K.op(dve, [RPS[pi]], [r_dst], lambda c4=c4, psb=psb: nc.vector.tensor_copy(
                    out=dstT[:, c4 * 4:(c4 + 1) * 4, tcol0:tcol0 + 128],
                    in_=psb[:, 0:512].rearrange("p (k n) -> p k n", k=4)))
            else:
                K.op(act, [RPS[pi]], [r_dst], lambda c4=c4, psb=psb: nc.scalar.copy(
                    out=dstT[:, c4 * 4:(c4 + 1) * 4, tcol0:tcol0 + 128],
                    in_=psb[:, 0:512].rearrange("p (k n) -> p k n", k=4)))

    def phase_p1():
        with ExitStack() as es:
            XT = es.enter_context(nc.sbuf_tensor("XT", [128, 16, T], BF16))
            tab1 = es.enter_context(nc.sbuf_tensor("tab1", [128, 2, T], F32))
            wr = es.enter_context(nc.sbuf_tensor("wr", [128, 3, 16, 512], BF16))
            stg = es.enter_context(nc.sbuf_tensor("stg", [128, 2, T], BF16))
            xin = es.enter_context(nc.sbuf_tensor("xin", [128, 2, D], F32))
            xb16 = es.enter_context(nc.sbuf_tensor("xb16", [128, 2, D], BF16))
            t12 = es.enter_context(nc.sbuf_tensor("t12", [128, 4, 512], F32))
            vst = es.enter_context(nc.sbuf_tensor("vst", [128, 2, 512], BF16))
            r_XT = [R("XT", j) for j in range(NT)]
            r_tab1 = R("tab1")
            K.dma(sp, [R("ropetab")], [r_tab1], lambda: nc.sync.dma_start(out=tab1[:], in_=ropetab[:, 0:2, :]))
            for j in range(NT):
                b = j % 2
                K.dma(sp, [], [R("xin", b)], lambda j=j, b=b: nc.sync.dma_start(
                    out=xin[:, b, :], in_=x_own[j * 128:(j + 1) * 128, :]))
                K.op(pool, [R("xin", b)], [R("xb16", b)], lambda b=b: nc.gpsimd.tensor_copy(
                    out=xb16[:, b, :], in_=xin[:, b, :]))
                transpose_rows(xb16[:, b, :], R("xb16", b), XT, r_XT[j], j * 128)
            wi = [0]

            def wslot():
                s = wi[0] % 3
                wi[0] += 1
                return s

            si = [0]
            blocks = []
            for hb in range(2):
                blocks.append(("qa", hb * 512, None, hb, SC128))
            for hb in range(2):
                blocks.append(("ka", 1024 + hb * 512, None, hb, 1.0))
            for hb in range(2):
                blocks.append(("qd", 3072 + hb * 512, hb * 512, hb, SC128))
            for hb in range(2):
                blocks.append(("kd", 4096 + hb * 512, 1024 + hb * 512, hb, 1.0))
            for (kind, c0, cr, hb, scale) in blocks:
                sa = wslot()
                load_wblock(wr[:, sa], R("wr", sa), w0, c0, 512)
                if cr is not None:
                    sb = wslot()
                    load_wblock(wr[:, sb], R("wr", sb), w0r, cr, 512)
                for k in range(4):
                    ch = hb * 4 + k
                    st = si[0] % 2
                    si[0] += 1
                    r_st = R("stg", st)
                    for tg in range(NG):
                        pa = next_ps()
                        def fa(sa=sa, k=k, tg=tg, pa=pa):
                            ins = None
                            for c in range(16):
                                ins = mm(PS[pa][:], wr[:, sa, c, k * 128:(k + 1) * 128], XT[:, c, tg * 512:(tg + 1) * 512],
                                         c == 0, c == 15)
                            return ins
                        K.op(pe, [R("wr", sa)] + r_XT[tg * 4:tg * 4 + 4], [RPS[pa]], fa)
                        if cr is None:
                            if tg % 2 == 0:
                                K.op(act, [RPS[pa]], [r_st], lambda st=st, tg=tg, pa=pa, scale=scale: nc.scalar.mul(
                                    out=stg[:, st, tg * 512:(tg + 1) * 512], in_=PS[pa][:], mul=scale))
                            else:
                                K.op(dve, [RPS[pa]], [r_st], lambda st=st, tg=tg, pa=pa, scale=scale: nc.vector.tensor_scalar(
                                    out=stg[:, st, tg * 512:(tg + 1) * 512], in0=PS[pa][:], scalar1=scale, scalar2=None,
                                    op0=ALU.mult))
                        else:
                            pb = next_ps()
                            def fb(sb=sb, k=k, tg=tg, pb=pb):
                                ins = None
                                for c in range(16):
                                    ins = mm(PS[pb][:], wr[:, sb, c, k * 128:(k + 1) * 128], XT[:, c, tg * 512:(tg + 1) * 512],
                                             c == 0, c == 15)
                                return ins
                            K.op(pe, [R("wr", sb)] + r_XT[tg * 4:tg * 4 + 4], [RPS[pb]], fb)
                            ta = (tg % 2) * 2
                            K.op(dve, [RPS[pa], r_tab1], [R("t12", ta)], lambda ta=ta, tg=tg, pa=pa: nc.vector.tensor_tensor(
                                out=t12[:, ta, :], in0=PS[pa][:], in1=tab1[:, 0, tg * 512:(tg + 1) * 512], op=ALU.mult))
                            K.op(dve, [RPS[pb], r_tab1], [R("t12", ta + 1)], lambda ta=ta, tg=tg, pb=pb: nc.vector.tensor_tensor(
                                out=t12[:, ta + 1, :], in0=PS[pb][:], in1=tab1[:, 1, tg * 512:(tg + 1) * 512], op=ALU.mult))
                            K.op(pool, [R("t12", ta), R("t12", ta + 1)], [R("t12", ta)], lambda ta=ta: nc.gpsimd.tensor_tensor(
                                out=t12[:, ta, :], in0=t12[:, ta, :], in1=t12[:, ta + 1, :], op=ALU.add))
                            K.op(act, [R("t12", ta)], [r_st], lambda st=st, tg=tg, ta=ta, scale=scale: nc.scalar.mul(
                                out=stg[:, st, tg * 512:(tg + 1) * 512], in_=t12[:, ta, :], mul=scale))
                    if kind == "qa":
                        K.dma(sp, [r_st], [R("qa_d", ch)], lambda st=st, ch=ch: nc.sync.dma_start(out=qa_d[ch], in_=stg[:, st, :]))
                    elif kind == "qd":
                        K.dma(sp, [r_st], [R("qd_d", ch)], lambda st=st, ch=ch: nc.sync.dma_start(out=qd_d[ch], in_=stg[:, st, :]))
                    elif kind == "ka":
                        K.dma(sp, [r_st], [R("kx_own", ch)], lambda st=st, ch=ch: nc.sync.dma_start(
                            out=kx_own[ch // 4][(ch % 4) * 128:(ch % 4 + 1) * 128, :], in_=stg[:, st, :]))
                    else:
                        K.dma(sp, [r_st], [R("kx_own", 8 + ch)], lambda st=st, ch=ch: nc.sync.dma_start(
                            out=kx_own[2 + ch // 4][(ch % 4) * 128:(ch % 4 + 1) * 128, :], in_=stg[:, st, :]))
            vi = [0]
            for nb, c0 in enumerate([2048, 2560, 5120, 5632]):
                sa = wslot()
                load_wblock(wr[:, sa], R("wr", sa), w0, c0, 512)
                for j in range(NT):
                    pa = next_ps()
                    def fv(sa=sa, j=j, pa=pa):
                        ins = None
                        for c in range(16):
                            ins = mm(PS[pa][:], XT[:, c, j * 128:(j + 1) * 128], wr[:, sa, c, :], c == 0, c == 15)
                        return ins
                    K.op(pe, [R("wr", sa), r_XT[j]], [RPS[pa]], fv)
                    vs = vi[0] % 2
                    vi[0] += 1
                    if vs == 0:
                        K.op(act, [RPS[pa]], [R("vst", vs)], lambda vs=vs, pa=pa: nc.scalar.copy(out=vst[:, vs, :], in_=PS[pa][:]))
                    else:
                        K.op(dve, [RPS[pa]], [R("vst", vs)], lambda vs=vs, pa=pa: nc.vector.tensor_copy(out=vst[:, vs, :], in_=PS[pa][:]))
                    K.dma(sp, [R("vst", vs)], [R("v_own", nb, j)], lambda vs=vs, j=j, nb=nb: nc.sync.dma_start(
                        out=v_own[j // 4][(j % 4) * 128:(j % 4 + 1) * 128, nb * 512:(nb + 1) * 512], in_=vst[:, vs, :]))
            for i in range(4):
                K.op(pool, [R("kx_own", 4 * i + k) for k in range(4)], [R("kx_all", i)], lambda i=i: nc.gpsimd.collective_compute(
                    "AllGather", ALU.bypass, replica_groups=groups, ins=[kx_own[i].opt()], outs=[kx_all[i].opt()]))
            for i in range(4):
                K.op(pool, [R("v_own", nb, j) for nb in range(4) for j in range(4 * i, 4 * i + 4)], [R("v_all", i)],
                     lambda i=i: nc.gpsimd.collective_compute(
                         "AllGather", ALU.bypass, replica_groups=groups, ins=[v_own[i].opt()], outs=[v_all[i].opt()]))
            K.barrier()

    def key_iter(G):
        out = []
        for kt in range(8 * G + 8):
            rel = kt - 8 * G
            col0 = 128 * (rel // 2) if rel >= 0 else 0
            r, j = loc_of(kt)
            out.append((kt, col0, rel if rel >= 0 else None, r, j))
        return out

    def phase_p2():
        with ExitStack() as es:
            msk = es.enter_context(nc.sbuf_tensor("msk", [128, 2, 8, 512], BF16))
            qh = es.enter_context(nc.sbuf_tensor("qh", [128, 2, 2, T], BF16))
            kh = es.enter_context(nc.sbuf_tensor("kh", [128, 2, 2, 2 * T], BF16))
            vh = es.enter_context(nc.sbuf_tensor("vh", [128, 2, 32, 256], BF16))
            wE = es.enter_context(nc.sbuf_tensor("wE", [128, 3, 512], F32))
            wSPf = es.enter_context(nc.sbuf_tensor("wSPf", [128, 2, 512], F32))
            wSP = es.enter_context(nc.sbuf_tensor("wSP", [128, 3, 512], BF16))
            wTMP = es.enter_context(nc.sbuf_tensor("wTMP", [128, 3, 512], F32))
            wW = es.enter_context(nc.sbuf_tensor("wW", [128, 4, 512], BF16))
            carry = es.enter_context(nc.sbuf_tensor("carry", [128, 512], F32))
            ost = es.enter_context(nc.sbuf_tensor("ost", [128, 2, 512], BF16))
            fin = es.enter_context(nc.sbuf_tensor("fin", [128, 8, 512], F32))
            sqb = es.enter_context(nc.sbuf_tensor("sqb", [128, 2, 512], BF16))
            accd = es.enter_context(nc.sbuf_tensor("accd", [128, 2, 512], F32))
            wD = es.enter_context(nc.sbuf_tensor("wD", [128, 6, 512], BF16))
            hld = es.enter_context(nc.sbuf_tensor("hld", [128, 2, 2, 512], BF16))
            r_msk = R("msk")
            K.dma(sp, [], [r_msk], lambda: nc.sync.dma_start(out=msk[:], in_=masks_d))
            oi = [0]
            MINI = os.environ.get("KMINI", "")
            for h in range(8 if not MINI else (1 if "s" in MINI else 0)):
                hb = h % 2
                r_q, r_k, r_v = R("qh", hb), R("kh", hb), R("vh", hb)
                K.dma(sp, [R("qa_d", h)], [r_q], lambda h=h, hb=hb: nc.sync.dma_start(out=qh[:, hb, 0, :], in_=qa_d[h]))
                for rr_ in range(2):
                    K.dma(sp, [R("kx_all", h // 4)], [r_k], lambda h=h, hb=hb, rr_=rr_: nc.sync.dma_start(
                        out=kh[:, hb, 0, rr_ * T:(rr_ + 1) * T],
                        in_=kx_all[h // 4][rr_ * 512 + (h % 4) * 128:rr_ * 512 + (h % 4 + 1) * 128, :]))
                    for i in range(4):
                        K.dma(sp, [R("v_all", i)], [r_v], lambda h=h, hb=hb, rr_=rr_, i=i: nc.sync.dma_start(
                            out=vh[:, hb, rr_ * 16 + i * 4:rr_ * 16 + i * 4 + 4, 0:128],
                            in_=v_all[i][rr_ * 512:(rr_ + 1) * 512, h * 128:(h + 1) * 128].rearrange("(l p) d -> p l d", p=128)))
                for G in range(NG):
                    conv_step(0)
                    keys = key_iter(G)[::-1]
                    n = len(keys)
                    r_carry = R("carry")
                    K.op(pool, [], [r_carry], lambda: nc.gpsimd.memset(carry[:], 0.0))
                    OB = 7
                    K.op(pe, [r_const, r_q], [RPS[OB]], lambda hb=hb, G=G: mm(
                        PS[OB][:], zeros[:], qh[:, hb, 0, G * 512:(G + 1) * 512], True, False))
                    for it in range(n + 2):
                        if it < n:
                            kt, col0, rel, r, j = keys[it]
                            zb = it % 3
                            cs = slice(col0, 512)
                            kcol = r * T + j * 128
                            K.op(pe, [r_k, r_q], [RPS[zb]], lambda hb=hb, G=G, zb=zb, cs=cs, kcol=kcol, col0=col0: mm(
                                PS[zb][:, cs], kh[:, hb, 0, kcol:kcol + 128], qh[:, hb, 0, G * 512 + col0:(G + 1) * 512], True, False))
                            K.op(act, [RPS[zb]], [R("wE", zb)], lambda zb=zb, cs=cs: nc.scalar.activation(
                                out=wE[:, zb, cs], in_=PS[zb][:, cs], func=AF.Exp))
                            if rel is None:
                                K.op(act, [R("wE", zb)], [R("wSP", zb)], lambda zb=zb, cs=cs: nc.scalar.activation(
                                    out=wSP[:, zb, cs], in_=wE[:, zb, cs], func=AF.Ln, bias=1.0))
                            else:
                                fb = it % 2
                                K.op(act, [R("wE", zb)], [R("wSPf", fb)], lambda zb=zb, cs=cs, fb=fb: nc.scalar.activation(
                                    out=wSPf[:, fb, cs], in_=wE[:, zb, cs], func=AF.Ln, bias=1.0))
                                K.op(pool, [R("wSPf", fb), r_msk], [R("wSP", zb)], lambda zb=zb, cs=cs, fb=fb, rel=rel: nc.gpsimd.tensor_tensor(
                                    out=wSP[:, zb, cs], in0=wSPf[:, fb, cs], in1=msk[:, 1, rel, cs], op=ALU.mult))
                        if 1 <= it <= n:
                            kt, col0, rel, r, j = keys[it - 1]
                            zb = (it - 1) % 3
                            cb = 3 + (it - 1) % 2
                            wb = (it - 1) % 2
                            cs = slice(col0, 512)
                            K.op(pe, [R("wSP", zb), r_const], [RPS[zb]], lambda zb=zb, cs=cs: mm(
                                PS[zb][:, cs], uneg, wSP[:, zb, cs], False, True))
                            K.op(pe, [R("wSP", zb), r_const], [RPS[cb]], lambda zb=zb, cb=cb, cs=cs: mm(
                                PS[cb][:, cs], negones[:], wSP[:, zb, cs], True, True))
                            K.op(dve, [RPS[zb], r_carry], [R("wTMP", zb)], lambda zb=zb, cs=cs: nc.vector.tensor_tensor(
                                out=wTMP[:, zb, cs], in0=PS[zb][:, cs], in1=carry[:, cs], op=ALU.add))
                            if rel is None:
                                K.op(act, [R("wTMP", zb)], [R("wW", wb)], lambda zb=zb, wb=wb, cs=cs: nc.scalar.activation(
                                    out=wW[:, wb, cs], in_=wTMP[:, zb, cs], func=AF.Exp))
                            else:
                                K.op(act, [R("wTMP", zb)], [R("wW", 2 + wb)], lambda zb=zb, wb=wb, cs=cs: nc.scalar.activation(
                                    out=wW[:, 2 + wb, cs], in_=wTMP[:, zb, cs], func=AF.Exp))
                                K.op(pool, [R("wW", 2 + wb), r_msk], [R("wW", wb)], lambda wb=wb, cs=cs, rel=rel: nc.gpsimd.tensor_tensor(
                                    out=wW[:, wb, cs], in0=wW[:, 2 + wb, cs], in1=msk[:, 1, rel, cs], op=ALU.mult))
                            K.op(dve, [RPS[cb], r_carry], [r_carry], lambda cb=cb, cs=cs: nc.vector.tensor_tensor(
                                out=carry[:, cs], in0=PS[cb][:, cs], in1=carry[:, cs], op=ALU.add))
                        if it >= 2:
                            kt, col0, rel, r, j = keys[it - 2]
                            wb = (it - 2) % 2
                            cs = slice(col0, 512)
                            l = r * 16 + j
                            last = (it - 2 == n - 1)
                            K.op(pe, [R("wW", wb), r_v], [RPS[OB]], lambda hb=hb, wb=wb, cs=cs, l=l, last=last: mm(
                                PS[OB][:, cs], vh[:, hb, l, 0:128], wW[:, wb, cs], False, last))
                    os_ = oi[0] % 2
                    oi[0] += 1
                    K.op(act, [RPS[OB]], [R("ost", os_)], lambda os_=os_: nc.scalar.copy(out=ost[:, os_, :], in_=PS[OB][:]))
                    K.dma(sp, [R("ost", os_)], [R("att", h, G)], lambda os_=os_, h=h, G=G: nc.sync.dma_start(
                        out=att_d[h][:, G * 512:(G + 1) * 512], in_=ost[:, os_, :]))
            for hd in range(4 if not MINI else (1 if "d" in MINI else 0)):
                hb = hd % 2
                r_q, r_k, r_v = R("qh", hb), R("kh", hb), R("vh", hb)
                for m in range(2):
                    K.dma(sp, [R("qd_d", 2 * hd + m)], [r_q], lambda hd=hd, hb=hb, m=m: nc.sync.dma_start(
                        out=qh[:, hb, m, :], in_=qd_d[2 * hd + m]))
                    for rr_ in range(2):
                        ch = 2 * hd + m
                        K.dma(sp, [R("kx_all", 2 + ch // 4)], [r_k], lambda ch=ch, hb=hb, m=m, rr_=rr_: nc.sync.dma_start(
                            out=kh[:, hb, m, rr_ * T:(rr_ + 1) * T],
                            in_=kx_all[2 + ch // 4][rr_ * 512 + (ch % 4) * 128:rr_ * 512 + (ch % 4 + 1) * 128, :]))
                for rr_ in range(2):
                    for i in range(4):
                        K.dma(sp, [R("v_all", i)], [r_v], lambda hd=hd, hb=hb, rr_=rr_, i=i: nc.sync.dma_start(
                            out=vh[:, hb, rr_ * 16 + i * 4:rr_ * 16 + i * 4 + 4, :],
                            in_=v_all[i][rr_ * 512:(rr_ + 1) * 512, 1024 + hd * 256:1024 + (hd + 1) * 256].rearrange("(l p) d -> p l d", p=128)))
                for G in range(NG):
                    conv_step(0)
                    keys = key_iter(G)
                    n = len(keys)
                    for bk in range(2, 6):
                        K.op(pe, [r_const, r_q], [RPS[bk]], lambda hb=hb, G=G, bk=bk: mm(
                            PS[bk][:], zeros[:], qh[:, hb, 0, G * 512:(G + 1) * 512], True, False))
                    K.op(dve, [], [R("accd", 0)], lambda: nc.vector.memset(accd[:, 0, :], 0.0))
                    K.op(dve, [], [R("accd", 1)], lambda: nc.vector.memset(accd[:, 1, :], 0.0))
                    for it in range(n + 2):
                        if it < n:
                            kt, col0, rel, r, j = keys[it]
                            cs = slice(col0, 512)
                            kcol = r * T + j * 128
                            for m in range(2):
                                wb = (it % 3) * 2 + m
                                sb_ = m + 6 * (it % 2)
                                K.op(pe, [r_k, r_q], [RPS[sb_]], lambda hb=hb, G=G, m=m, cs=cs, kcol=kcol, col0=col0, sb_=sb_: mm(
                                    PS[sb_][:, cs], kh[:, hb, m, kcol:kcol + 128], qh[:, hb, m, G * 512 + col0:(G + 1) * 512], True, True))
                                if rel is None:
                                    K.op(act, [RPS[sb_]], [R("wD", wb)], lambda sb_=sb_, wb=wb, cs=cs: nc.scalar.activation(
                                        out=wD[:, wb, cs], in_=PS[sb_][:, cs], func=AF.Exp))
                                else:
                                    K.op(act, [RPS[sb_]], [R("wSP", m)], lambda sb_=sb_, m=m, cs=cs: nc.scalar.activation(
                                        out=wSP[:, m, cs], in_=PS[sb_][:, cs], func=AF.Exp))
                                    K.op(pool, [R("wSP", m), r_msk], [R("wD", wb)], lambda m=m, wb=wb, cs=cs, rel=rel: nc.gpsimd.tensor_tensor(
                                        out=wD[:, wb, cs], in0=wSP[:, m, cs], in1=msk[:, 0, rel, cs], op=ALU.mult))
                        if it >= 2:
                            kt, col0, rel, r, j = keys[it - 2]
                            cs = slice(col0, 512)
                            l = r * 16 + j
                            last = (it - 2 == n - 1)
                            for m in range(2):
                                wb = ((it - 2) % 3) * 2 + m
                                for a in range(2):
                                    bk = 2 + 2 * m + a
                                    K.op(pe, [R("wD", wb), r_v], [RPS[bk]], lambda hb=hb, wb=wb, cs=cs, l=l, last=last, a=a, bk=bk: mm(
                                        PS[bk][:, cs], vh[:, hb, l, a * 128:(a + 1) * 128], wD[:, wb, cs], False, last))
                                K.op(dve, [R("wD", wb), R("accd", m)], [R("accd", m)], lambda wb=wb, cs=cs, m=m: nc.vector.tensor_tensor(
                                    out=accd[:, m, cs], in0=accd[:, m, cs], in1=wD[:, wb, cs], op=ALU.add))
                    for m in range(2):
                        K.op(act, [R("accd", m)], [R("hld", m, 0)], lambda m=m: nc.scalar.copy(out=hld[:, m, 0, :], in_=accd[:, m, :]))
                        K.op(dve, [R("accd", m), R("hld", m, 0)], [R("hld", m, 1)], lambda m=m: nc.vector.tensor_tensor(
                            out=hld[:, m, 1, :], in0=accd[:, m, :], in1=hld[:, m, 0, :], op=ALU.subtract))
                        K.op(pe, [R("hld", m, 0), R("hld", m, 1), r_const], [RPS[6 + m]], lambda m=m: (
                            mm(PS[6 + m][:], ones[:], hld[:, m, 0, :], True, False),
                            mm(PS[6 + m][:], ones[:], hld[:, m, 1, :], False, True))[1])
                    r_fin = [R("fin", i) for i in range(8)]
                    K.op(dve, [RPS[6]], [r_fin[0]], lambda: nc.vector.reciprocal(out=fin[:, 0, :], in_=PS[6][:]))
                    K.op(dve, [RPS[7]], [r_fin[1]], lambda: nc.vector.reciprocal(out=fin[:, 1, :], in_=PS[7][:]))
                    K.op(dve, [r_fin[1], r_const], [r_fin[1]], lambda: nc.vector.tensor_scalar(
                        out=fin[:, 1, :], in0=fin[:, 1, :], scalar1=lamc[:, 0:1], scalar2=None, op0=ALU.mult))
                    for a in range(2):
                        K.op(dve, [RPS[2 + a], r_fin[0]], [r_fin[2 + a]], lambda a=a: nc.vector.tensor_tensor(
                            out=fin[:, 2 + a, :], in0=PS[2 + a][:], in1=fin[:, 0, :], op=ALU.mult))
                        K.op(dve, [RPS[4 + a], r_fin[1]], [r_fin[4 + a]], lambda a=a: nc.vector.tensor_tensor(
                            out=fin[:, 4 + a, :], in0=PS[4 + a][:], in1=fin[:, 1, :], op=ALU.mult))
                        K.op(pool, [r_fin[2 + a], r_fin[4 + a]], [r_fin[2 + a]], lambda a=a: nc.gpsimd.tensor_tensor(
                            out=fin[:, 2 + a, :], in0=fin[:, 2 + a, :], in1=fin[:, 4 + a, :], op=ALU.add))
                        K.op(act, [r_fin[2 + a]], [R("sqb", a)], lambda a=a: nc.scalar.activation(
                            out=sqb[:, a, :], in_=fin[:, 2 + a, :], func=AF.Square))
                    K.op(pe, [R("sqb", 0), R("sqb", 1), r_const], [RPS[0]], lambda: (
                        mm(PS[0][:], ones[:], sqb[:, 0, :], True, False), mm(PS[0][:], ones[:], sqb[:, 1, :], False, True))[1])
                    K.op(act, [RPS[0]], [r_fin[6]], lambda: nc.scalar.activation(
                        out=fin[:, 6, :], in_=PS[0][:], func=AF.Sqrt, bias=epsc[:, 0:1], scale=1.0 / 256.0))
                    K.op(dve, [r_fin[6]], [r_fin[6]], lambda: nc.vector.reciprocal(out=fin[:, 6, :], in_=fin[:, 6, :]))
                    for a in range(2):
                        os_ = oi[0] % 2
                        oi[0] += 1
                        K.op(dve, [r_fin[2 + a], r_fin[6], r_const], [R("ost", os_)], lambda a=a, os_=os_: nc.vector.scalar_tensor_tensor(
                            out=ost[:, os_, :], in0=fin[:, 2 + a, :], scalar=subg[:, a:a + 1], in1=fin[:, 6, :],
                            op0=ALU.mult, op1=ALU.mult))
                        K.dma(sp, [R("ost", os_)], [R("att", 8 + 2 * hd + a, G)], lambda os_=os_, hd=hd, a=a, G=G: nc.sync.dma_start(
                            out=att_d[8 + 2 * hd + a][:, G * 512:(G + 1) * 512], in_=ost[:, os_, :]))
            K.barrier()

    def phase_mlp(layer, w_out_d, xin_d, xin_is_input, xmid_d, xout_d, write_xT):
        with ExitStack() as es:
            ATX = es.enter_context(nc.sbuf_tensor("ATX%d" % layer, [128, 16, 512], BF16))
            WB = es.enter_context(nc.sbuf_tensor("WB%d" % layer, [128, 4, 16, 512], BF16))
            XH = es.enter_context(nc.sbuf_tensor("XH%d" % layer, [128, 4, D], F32))
            HT = es.enter_context(nc.sbuf_tensor("HT%d" % layer, [128, 44, 512], BF16))
            GB = es.enter_context(nc.sbuf_tensor("GB%d" % layer, [128, 2, D], F32))
            xb16 = es.enter_context(nc.sbuf_tensor("xb16m%d" % layer, [128, 2, D], BF16))
            sg = es.enter_context(nc.sbuf_tensor("sg%d" % layer, [128, 2, 512], F32))
            st6 = es.enter_context(nc.sbuf_tensor("st6%d" % layer, [128, 24], F32))
            mv = es.enter_context(nc.sbuf_tensor("mv%d" % layer, [128, 4], F32))
            wi = [0]

            def wslot():
                s = wi[0] % 4
                wi[0] += 1
                return s

            xi = [0]
            conv_flush(layer)

            def layer_norm_tile(tt, lnidx, r_xh):
                for q in range(4):
                    K.op(dve, [r_xh], [R("st6")], lambda q=q, tt=tt: nc.vector.bn_stats(
                        out=st6[:, q * 6:(q + 1) * 6], in_=XH[:, tt, q * 512:(q + 1) * 512]))
                K.op(dve, [R("st6")], [R("mv")], lambda: nc.vector.bn_aggr(out=mv[:, 0:2], in_=st6[:]))
                K.op(act, [R("mv")], [R("mv")], lambda: nc.scalar.activation(
                    out=mv[:, 2:3], in_=mv[:, 1:2], func=AF.Sqrt, bias=epsc[:, 0:1], scale=1.0))
                K.op(dve, [R("mv")], [R("mv")], lambda: nc.vector.reciprocal(out=mv[:, 2:3], in_=mv[:, 2:3]))
                K.op(dve, [r_xh, R("mv")], [r_xh], lambda tt=tt: nc.vector.tensor_scalar(
                    out=XH[:, tt, :], in0=XH[:, tt, :], scalar1=mv[:, 0:1], scalar2=mv[:, 2:3],
                    op0=ALU.subtract, op1=ALU.mult))
                K.op(dve, [r_xh, R("GB")], [r_xh], lambda tt=tt: nc.vector.tensor_tensor(
                    out=XH[:, tt, :], in0=XH[:, tt, :], in1=GB[:, 0, :], op=ALU.mult))
                K.op(dve, [r_xh, R("GB")], [r_xh], lambda tt=tt: nc.vector.tensor_tensor(
                    out=XH[:, tt, :], in0=XH[:, tt, :], in1=GB[:, 1, :], op=ALU.add))

            def load_gb(lnidx):
                K.dma(sp, [], [R("GB")], lambda: nc.sync.dma_start(
                    out=GB[:, 0, :], in_=lng[lnidx:lnidx + 1, :].partition_broadcast(128)))
                K.dma(sp, [], [R("GB")], lambda: nc.sync.dma_start(
                    out=GB[:, 1, :], in_=lnb[lnidx:lnidx + 1, :].partition_broadcast(128)))

            for tg in range(NG):
                r_atx = R("ATX")
                r_xh = [R("XH", tt) for tt in range(4)]
                K.dma(sp, [R("att", c, tg) for c in range(16)], [r_atx], lambda tg=tg: nc.sync.dma_start(
                    out=ATX[:], in_=att_d[:, :, tg * 512:(tg + 1) * 512].rearrange("c p t -> p c t")))
                for tt in range(4):
                    row0 = (tg * 4 + tt) * 128
                    K.dma(sp, [R("xres", id(xin_d), tg * 4 + tt)], [r_xh[tt]], lambda tt=tt, row0=row0: nc.sync.dma_start(
                        out=XH[:, tt, :], in_=xin_d[row0:row0 + 128, :]))
                load_gb(2 * layer)
                for nb in range(4):
                    sa = wslot()
                    load_wblock(WB[:, sa], R("WB", sa), wob[layer], nb * 512, 512, deps=conv_res[layer]["wo"])
                    for tt in range(4):
                        pa = next_ps()
                        def fo(sa=sa, tt=tt, pa=pa):
                            ins = None
                            for c in range(16):
                                ins = mm(PS[pa][:], ATX[:, c, tt * 128:(tt + 1) * 128], WB[:, sa, c, :], c == 0, c == 15)
                            return ins
                        K.op(pe, [R("WB", sa), r_atx], [RPS[pa]], fo)
                        K.op(dve, [RPS[pa], r_xh[tt]], [r_xh[tt]], lambda tt=tt, nb=nb, pa=pa: nc.vector.scalar_tensor_tensor(
                            out=XH[:, tt, nb * 512:(nb + 1) * 512], in0=XH[:, tt, nb * 512:(nb + 1) * 512], scalar=ALPHA,
                            in1=PS[pa][:], op0=ALU.mult, op1=ALU.add))
                for tt in range(4):
                    layer_norm_tile(tt, 2 * layer, r_xh[tt])
                    b = xi[0] % 2
                    xi[0] += 1
                    K.op(act, [r_xh[tt]], [R("xb16m", b)], lambda tt=tt, b=b: nc.scalar.copy(out=xb16[:, b, :], in_=XH[:, tt, :]))
                    transpose_rows(xb16[:, b, :], R("xb16m", b), ATX, r_atx, tt * 128)
                load_gb(2 * layer + 1)
                for fb in range(NFB):
                    sg_ = wslot()
                    load_wblock(WB[:, sg_], R("WB", sg_), fgb[layer], fb * 512, 512, deps=conv_res[layer]["fg"])
                    su = wslot()
                    load_wblock(WB[:, su], R("WB", su), fub[layer], fb * 512, 512, deps=conv_res[layer]["fu"])
                    for k in range(4):
                        fc = fb * 4 + k
                        pg = next_ps()
                        pu = next_ps()
                        def fgm(s=sg_, k=k, p=pg):
                            ins = None
                            for c in range(16):
                                ins = mm(PS[p][:], WB[:, s, c, k * 128:(k + 1) * 128], ATX[:, c, :], c == 0, c == 15)
                            return ins
                        def fum(s=su, k=k, p=pu):
                            ins = None
                            for c in range(16):
                                ins = mm(PS[p][:], WB[:, s, c, k * 128:(k + 1) * 128], ATX[:, c, :], c == 0, c == 15)
                            return ins
                        K.op(pe, [R("WB", sg_), r_atx], [RPS[pg]], fgm)
                        K.op(pe, [R("WB", su), r_atx], [RPS[pu]], fum)
                        sb = fc % 2
                        K.op(act, [RPS[pg]], [R("sg", sb)], lambda sb=sb, pg=pg: nc.scalar.activation(
                            out=sg[:, sb, :], in_=PS[pg][:], func=AF.Silu))
                        K.op(dve, [R("sg", sb), RPS[pu]], [R("HT", fc)], lambda sb=sb, pu=pu, fc=fc: nc.vector.tensor_tensor(
                            out=HT[:, fc, :], in0=sg[:, sb, :], in1=PS[pu][:], op=ALU.mult))
                for nb in range(4):
                    banks = [next_ps() for _ in range(4)]
                    for sl in range(4):
                        sa = wslot()
                        load_wblock(WB[:, sa], R("WB", sa), fdb[layer], nb * 512, 512, nchunks=11, row0=sl * 11 * 128,
                                    deps=conv_res[layer]["fd"][2 * sl:2 * sl + 2])
                        for tt in range(4):
                            pa = banks[tt]
                            def fdm(sa=sa, sl=sl, tt=tt, pa=pa):
                                ins = None
                                for c in range(11):
                                    ins = mm(PS[pa][:], HT[:, sl * 11 + c, tt * 128:(tt + 1) * 128], WB[:, sa, c, :],
                                             sl == 0 and c == 0, sl == 3 and c == 10)
                                return ins
                            K.op(pe, [R("WB", sa)] + [R("HT", sl * 11 + c) for c in range(11)], [RPS[pa]], fdm)
                    for tt in range(4):
                        pa = banks[tt]
                        K.op(dve, [RPS[pa], r_xh[tt]], [r_xh[tt]], lambda tt=tt, nb=nb, pa=pa: nc.vector.scalar_tensor_tensor(
                            out=XH[:, tt, nb * 512:(nb + 1) * 512], in0=XH[:, tt, nb * 512:(nb + 1) * 512], scalar=ALPHA,
                            in1=PS[pa][:], op0=ALU.mult, op1=ALU.add))
                for tt in range(4):
                    layer_norm_tile(tt, 2 * layer + 1, r_xh[tt])
                    row0 = (tg * 4 + tt) * 128
                    K.dma(sp, [r_xh[tt]], [R("xres", id(xout_d), tg * 4 + tt)], lambda tt=tt, row0=row0: nc.sync.dma_start(
                        out=xout_d[row0:row0 + 128, :], in_=XH[:, tt, :]))
                    if write_xT:
                        b = xi[0] % 2
                        xi[0] += 1
                        K.op(act, [r_xh[tt]], [R("xb16m", b)], lambda tt=tt, b=b: nc.scalar.copy(out=xb16[:, b, :], in_=XH[:, tt, :]))
                        transpose_rows(xb16[:, b, :], R("xb16m", b), ATX, r_atx, tt * 128)
                if write_xT:
                    K.dma(sp, [r_atx], [R("xT_d", tg)], lambda tg=tg: nc.sync.dma_start(
                        out=xT_d[:, :, tg * 512:(tg + 1) * 512].rearrange("c p t -> p c t"), in_=ATX[:]))
            K.barrier()

    def phase_p4():
        with ExitStack() as es:
            XT = es.enter_context(nc.sbuf_tensor("XT4", [128, 16, T], BF16))
            win = es.enter_context(nc.sbuf_tensor("win", [128, 16, 1088], BF16))
            winr = es.enter_context(nc.sbuf_tensor("winr", [128, 16, 64], BF16))
            tab4 = es.enter_context(nc.sbuf_tensor("tab4", [64, 2, T], F32))
            mgb = es.enter_context(nc.sbuf_tensor("mgb", [128, 1024], F32))
            cT = es.enter_context(nc.sbuf_tensor("cT", [128, 8, T], BF16))
            kpe = es.enter_context(nc.sbuf_tensor("kpe", [64, T], BF16))
            cn = es.enter_context(nc.sbuf_tensor("cn", [128, 2, 1024], BF16))
            junk = es.enter_context(nc.sbuf_tensor("junk", [128, 512], F32))
            ss = es.enter_context(nc.sbuf_tensor("ss", [128, 4], F32))
            t4 = es.enter_context(nc.sbuf_tensor("t4", [64, 2, 512], F32))
            r_XT = [R("XT4", tg) for tg in range(NG)]
            for tg in range(NG):
                K.dma(sp, [R("xT_d", tg)], [r_XT[tg]], lambda tg=tg: nc.sync.dma_start(
                    out=XT[:, :, tg * 512:(tg + 1) * 512], in_=xT_d[:, :, tg * 512:(tg + 1) * 512].rearrange("c p t -> p c t")))
            for q in range(2):
                K.dma(pool, [], [R("win")], lambda q=q: nc.gpsimd.dma_start(
                    out=win[:, :, q * 544:(q + 1) * 544],
                    in_=mwin[:, q * 544:(q + 1) * 544].rearrange("(c p) n -> p c n", p=128)))
            K.dma(pool, [], [R("winr")], lambda: nc.gpsimd.dma_start(
                out=winr[:], in_=mwinr.rearrange("(c p) n -> p c n", p=128)))
            K.dma(sp, [R("ropetab")], [R("tab4")], lambda: nc.sync.dma_start(out=tab4[:], in_=ropetab[0:64, 2:4, :]))
            K.dma(sp, [], [R("mgb")], lambda: nc.sync.dma_start(out=mgb[:], in_=mg_d.partition_broadcast(128)))
            r_cT = R("cT")
            for j in range(NT):
                b = j % 2
                for half in range(2):
                    pa = next_ps()
                    def fc_(half=half, j=j, pa=pa):
                        ins = None
                        for c in range(16):
                            ins = mm(PS[pa][:], XT[:, c, j * 128:(j + 1) * 128], win[:, c, half * 512:(half + 1) * 512], c == 0, c == 15)
                        return ins
                    K.op(pe, [R("win"), r_XT[j // 4]], [RPS[pa]], fc_)
                    K.op(dve, [RPS[pa]], [R("ss")], lambda pa=pa: nc.vector.bn_stats(out=junk[:, 0:6], in_=PS[pa][:]))
                    K.op(dve, [R("ss")], [R("ss")], lambda: nc.vector.bn_aggr(out=junk[:, 8:10], in_=junk[:, 0:6]))
                    K.op(dve, [R("ss")], [R("ss")], lambda: nc.vector.scalar_tensor_tensor(
                        out=junk[:, 10:11], in0=junk[:, 8:9], scalar=junk[:, 8:9], in1=junk[:, 9:10], op0=ALU.mult, op1=ALU.add))
                    K.op(act, [R("ss")], [R("ss")], lambda half=half: nc.scalar.activation(
                        out=ss[:, 2 + half:3 + half], in_=junk[:, 10:11], func=AF.Sqrt, bias=epsc[:, 1:2], scale=1.0))
                    K.op(dve, [R("ss")], [R("ss")], lambda half=half: nc.vector.reciprocal(
                        out=ss[:, 2 + half:3 + half], in_=ss[:, 2 + half:3 + half]))
                    K.op(dve, [RPS[pa], R("ss"), R("mgb")], [R("cn", b)], lambda pa=pa, half=half, b=b: nc.vector.scalar_tensor_tensor(
                        out=cn[:, b, half * 512:(half + 1) * 512], in0=PS[pa][:], scalar=ss[:, 2 + half:3 + half],
                        in1=mgb[:, half * 512:(half + 1) * 512], op0=ALU.mult, op1=ALU.mult))
                for c4 in range(2):
                    pi = next_ps()
                    psb = PS[pi][:].bitcast(BF16)
                    def ft(c4=c4, psb=psb, b=b):
                        ins = None
                        for k in range(4):
                            c = c4 * 4 + k
                            ins = nc.tensor.transpose(psb[:, k * 128:(k + 1) * 128], cn[:, b, c * 128:(c + 1) * 128], ident)
                        return ins
                    K.op(pe, [R("cn", b), r_const], [RPS[pi]], ft)
                    K.op(dve, [RPS[pi]], [r_cT], lambda c4=c4, psb=psb, j=j: nc.vector.tensor_copy(
                        out=cT[:, c4 * 4:(c4 + 1) * 4, j * 128:(j + 1) * 128],
                        in_=psb[:, 0:512].rearrange("p (k n) -> p k n", k=4)))
            for tg in range(NG):
                pa = next_ps()
                pb = next_ps()
                def fa(tg=tg, pa=pa):
                    ins = None
                    for c in range(16):
                        ins = mm(PS[pa][0:64, :], win[:, c, 1024:1088], XT[:, c, tg * 512:(tg + 1) * 512], c == 0, c == 15)
                    return ins
                def fb_(tg=tg, pb=pb):
                    ins = None
                    for c in range(16):
                        ins = mm(PS[pb][0:64, :], winr[:, c, :], XT[:, c, tg * 512:(tg + 1) * 512], c == 0, c == 15)
                    return ins
                K.op(pe, [R("win"), r_XT[tg]], [RPS[pa]], fa)
                K.op(pe, [R("winr"), r_XT[tg]], [RPS[pb]], fb_)
                K.op(dve, [RPS[pa], R("tab4")], [R("t4", 0)], lambda tg=tg, pa=pa: nc.vector.tensor_tensor(
                    out=t4[:, 0, :], in0=PS[pa][0:64, :], in1=tab4[:, 0, tg * 512:(tg + 1) * 512], op=ALU.mult))
                K.op(dve, [RPS[pb], R("tab4")], [R("t4", 1)], lambda tg=tg, pb=pb: nc.vector.tensor_tensor(
                    out=t4[:, 1, :], in0=PS[pb][0:64, :], in1=tab4[:, 1, tg * 512:(tg + 1) * 512], op=ALU.mult))
                K.op(dve, [R("t4", 0), R("t4", 1)], [R("kpe")], lambda tg=tg: nc.vector.tensor_tensor(
                    out=kpe[:, tg * 512:(tg + 1) * 512], in0=t4[:, 0, :], in1=t4[:, 1, :], op=ALU.add))
            K.dma(sp, [r_cT], [R("cq_d")], lambda: nc.sync.dma_start(
                out=cq_d.rearrange("c p t -> p c t"), in_=cT[:, 0:4, :]))
            K.dma(sp, [r_cT], [R("mx_own", 0)], lambda: nc.sync.dma_start(
                out=mx_own[0].rearrange("(c p) t -> p c t", p=128), in_=cT[:, 4:8, :]))
            K.dma(sp, [R("kpe")], [R("mx_own", 1)], lambda: nc.sync.dma_start(out=mx_own[1], in_=kpe[:]))
            for i in range(2):
                K.op(pool, [R("mx_own", i)], [R("mx_all", i)], lambda i=i: nc.gpsimd.collective_compute(
                    "AllGather", ALU.bypass, replica_groups=groups, ins=[mx_own[i].opt()], outs=[mx_all[i].opt()]))
            K.barrier()

    def phase_p5():
        with ExitStack() as es:
            msk = es.enter_context(nc.sbuf_tensor("msk5", [128, 8, 512], BF16))
            ckv = es.enter_context(nc.sbuf_tensor("ckv", [128, 4, 2 * T], BF16))
            kpa = es.enter_context(nc.sbuf_tensor("kpa", [64, 2 * T], BF16))
            cqs = es.enter_context(nc.sbuf_tensor("cqs", [128, 4, T], BF16))
            tab5 = es.enter_context(nc.sbuf_tensor("tab5", [64, 2, T], F32))
            wk = es.enter_context(nc.sbuf_tensor("wk", [128, 4, 512], BF16))
            wv = es.enter_context(nc.sbuf_tensor("wv", [128, 4, 512], BF16))
            wqn = es.enter_context(nc.sbuf_tensor("wqn", [128, 4, 512], BF16))
            wqp = es.enter_context(nc.sbuf_tensor("wqp", [128, 4, 256], BF16))
            wqpr = es.enter_context(nc.sbuf_tensor("wqpr", [128, 4, 256], BF16))
            knT = es.enter_context(nc.sbuf_tensor("knT", [128, 4, 2 * T], BF16))
            V5 = es.enter_context(nc.sbuf_tensor("V5", [128, 32, 512], BF16))
            qn = es.enter_context(nc.sbuf_tensor("qn", [128, 2, T], BF16))
            qp = es.enter_context(nc.sbuf_tensor("qp", [64, 2, T], BF16))
            t5 = es.enter_context(nc.sbuf_tensor("t5", [64, 2, 512], F32))
            pp = es.enter_context(nc.sbuf_tensor("pp", [128, 4, 512], BF16))
            pm = es.enter_context(nc.sbuf_tensor("pm", [128, 2, 512], BF16))
            rr = es.enter_context(nc.sbuf_tensor("rr", [128, 512], F32))
            ost = es.enter_context(nc.sbuf_tensor("ost5", [128, 2, 512], BF16))
            acc5 = es.enter_context(nc.sbuf_tensor("acc5", [128, 2, 512], F32))
            hl5 = es.enter_context(nc.sbuf_tensor("hl5", [128, 2, 512], BF16))
            K.dma(sp, [], [R("msk5")], lambda: nc.sync.dma_start(out=msk[:], in_=masks_d[:, 0]))
            for rr_ in range(2):
                K.dma(sp, [R("mx_all", 0)], [R("ckv")], lambda rr_=rr_: nc.sync.dma_start(
                    out=ckv[:, :, rr_ * T:(rr_ + 1) * T],
                    in_=mx_all[0][rr_ * 512:(rr_ + 1) * 512, :].rearrange("(c p) t -> p c t", p=128)))
                K.dma(sp, [R("mx_all", 1)], [R("kpa")], lambda rr_=rr_: nc.sync.dma_start(
                    out=kpa[:, rr_ * T:(rr_ + 1) * T], in_=mx_all[1][rr_ * 64:(rr_ + 1) * 64, :]))
            K.dma(sp, [R("cq_d")], [R("cqs")], lambda: nc.sync.dma_start(out=cqs[:], in_=cq_d.rearrange("c p t -> p c t")))
            K.dma(sp, [R("ropetab")], [R("tab5")], lambda: nc.sync.dma_start(out=tab5[:], in_=ropetab[0:64, 2:4, :]))
            oi = [0]
            ci = [0]

            for hg in range(4):
                for (dst, rn, src, c0, nc_) in ((wk, "wk", mwkv, hg * 512, 512), (wv, "wv", mwkv, 2048 + hg * 512, 512),
                                                (wqn, "wqn", mwq, hg * 512, 512), (wqp, "wqp", mwq, 2048 + hg * 256, 256),
                                                (wqpr, "wqpr", mwqr, hg * 256, 256)):
                    load_wblock(dst, R(rn), src, c0, nc_, nchunks=4)
                for l in range(32):
                    pa = next_ps()
                    def fv(l=l, pa=pa):
                        ins = None
                        for c in range(4):
                            ins = mm(PS[pa][:], ckv[:, c, l * 128:(l + 1) * 128], wv[:, c, :], c == 0, c == 3)
                        return ins
                    K.op(pe, [R("ckv"), R("wv")], [RPS[pa]], fv)
                    if l % 2 == 0:
                        K.op(act, [RPS[pa]], [R("V5")], lambda l=l, pa=pa: nc.scalar.copy(out=V5[:, l, :], in_=PS[pa][:]))
                    else:
                        K.op(dve, [RPS[pa]], [R("V5")], lambda l=l, pa=pa: nc.vector.tensor_copy(out=V5[:, l, :], in_=PS[pa][:]))
                for hh in range(4):
                    for kg in range(8):
                        pa = next_ps()
                        def fk(hh=hh, kg=kg, pa=pa):
                            ins = None
                            for c in range(4):
                                ins = mm(PS[pa][:], wk[:, c, hh * 128:(hh + 1) * 128], ckv[:, c, kg * 512:(kg + 1) * 512], c == 0, c == 3)
                            return ins
                        K.op(pe, [R("ckv"), R("wk")], [RPS[pa]], fk)
                        if kg % 2 == 0:
                            K.op(act, [RPS[pa]], [R("knT")], lambda hh=hh, kg=kg, pa=pa: nc.scalar.copy(
                                out=knT[:, hh, kg * 512:(kg + 1) * 512], in_=PS[pa][:]))
                        else:
                            K.op(dve, [RPS[pa]], [R("knT")], lambda hh=hh, kg=kg, pa=pa: nc.vector.tensor_copy(
                                out=knT[:, hh, kg * 512:(kg + 1) * 512], in_=PS[pa][:]))
                for hh in range(4):
                    h = hg * 4 + hh
                    qb = h % 2
                    r_qn, r_qp = R("qn", qb), R("qp", qb)
                    for tg in range(NG):
                        pa, pb, pc = next_ps(), next_ps(), next_ps()
                        def fqn(hh=hh, tg=tg, pa=pa):
                            ins = None
                            for c in range(4):
                                ins = mm(PS[pa][:], wqn[:, c, hh * 128:(hh + 1) * 128], cqs[:, c, tg * 512:(tg + 1) * 512], c == 0, c == 3)
                            return ins
                        def fqa(hh=hh, tg=tg, pb=pb):
                            ins = None
                            for c in range(4):
                                ins = mm(PS[pb][0:64, :], wqp[:, c, hh * 64:(hh + 1) * 64], cqs[:, c, tg * 512:(tg + 1) * 512], c == 0, c == 3)
                            return ins
                        def fqb(hh=hh, tg=tg, pc=pc):
                            ins = None
                            for c in range(4):
                                ins = mm(PS[pc][0:64, :], wqpr[:, c, hh * 64:(hh + 1) * 64], cqs[:, c, tg * 512:(tg + 1) * 512], c == 0, c == 3)
                            return ins
                        K.op(pe, [R("cqs"), R("wqn")], [RPS[pa]], fqn)
                        K.op(pe, [R("cqs"), R("wqp")], [RPS[pb]], fqa)
                        K.op(pe, [R("cqs"), R("wqpr")], [RPS[pc]], fqb)
                        K.op(act, [RPS[pa]], [r_qn], lambda qb=qb, tg=tg, pa=pa: nc.scalar.mul(
                            out=qn[:, qb, tg * 512:(tg + 1) * 512], in_=PS[pa][:], mul=SC192))
                        K.op(dve, [RPS[pb], R("tab5")], [R("t5", 0)], lambda tg=tg, pb=pb: nc.vector.tensor_tensor(
                            out=t5[:, 0, :], in0=PS[pb][0:64, :], in1=tab5[:, 0, tg * 512:(tg + 1) * 512], op=ALU.mult))
                        K.op(dve, [RPS[pc], R("tab5")], [R("t5", 1)], lambda tg=tg, pc=pc: nc.vector.tensor_tensor(
                            out=t5[:, 1, :], in0=PS[pc][0:64, :], in1=tab5[:, 1, tg * 512:(tg + 1) * 512], op=ALU.mult))
                        K.op(dve, [R("t5", 0), R("t5", 1)], [R("t5", 0)], lambda: nc.vector.tensor_tensor(
                            out=t5[:, 0, :], in0=t5[:, 0, :], in1=t5[:, 1, :], op=ALU.add))
                        K.op(act, [R("t5", 0)], [r_qp], lambda qb=qb, tg=tg: nc.scalar.mul(
                            out=qp[:, qb, tg * 512:(tg + 1) * 512], in_=t5[:, 0, :], mul=SC192))
                    for G in range(NG):
                        conv_step(1)
                        keys = key_iter(G)
                        n = len(keys)
                        OB, RB = 6, 7
                        K.op(pe, [r_const, r_qn], [RPS[OB]], lambda qb=qb, G=G: mm(
                            PS[OB][:], zeros[:], qn[:, qb, G * 512:(G + 1) * 512], True, False))
                        K.op(dve, [], [R("acc5", 0)], lambda: nc.vector.memset(acc5[:, 0, :], 0.0))
                        for it in range(n + 2):
                            if it < n:
                                kt, col0, rel, r, j = keys[it]
                                cs = slice(col0, 512)
                                kcol = r * T + j * 128
                                sbk = it % 3
                                wb = it % 4
                                def fs(hh=hh, qb=qb, G=G, cs=cs, kcol=kcol, col0=col0, sbk=sbk):
                                    mm(PS[sbk][:, cs], knT[:, hh, kcol:kcol + 128], qn[:, qb, G * 512 + col0:(G + 1) * 512], True, False)
                                    return mm(PS[sbk][:, cs], kpa[:, kcol:kcol + 128], qp[:, qb, G * 512 + col0:(G + 1) * 512], False, True)
                                K.op(pe, [R("knT"), R("kpa"), r_qn, r_qp], [RPS[sbk]], fs)
                                if rel is None:
                                    K.op(act, [RPS[sbk]], [R("pp", wb)], lambda sbk=sbk, wb=wb, cs=cs: nc.scalar.activation(
                                        out=pp[:, wb, cs], in_=PS[sbk][:, cs], func=AF.Exp))
                                else:
                                    mb = it % 2
                                    K.op(act, [RPS[sbk]], [R("pm", mb)], lambda sbk=sbk, mb=mb, cs=cs: nc.scalar.activation(
                                        out=pm[:, mb, cs], in_=PS[sbk][:, cs], func=AF.Exp))
                                    K.op(pool, [R("pm", mb), R("msk5")], [R("pp", wb)], lambda mb=mb, wb=wb, cs=cs, rel=rel: nc.gpsimd.tensor_tensor(
                                        out=pp[:, wb, cs], in0=pm[:, mb, cs], in1=msk[:, rel, cs], op=ALU.mult))
                            if it >= 2:
                                kt, col0, rel, r, j = keys[it - 2]
                                cs = slice(col0, 512)
                                l = r * 16 + j
                                last = (it - 2 == n - 1)
                                wb = (it - 2) % 4
                                K.op(pe, [R("pp", wb), R("V5")], [RPS[OB]], lambda hh=hh, wb=wb, cs=cs, l=l, last=last: mm(
                                    PS[OB][:, cs], V5[:, l, hh * 128:(hh + 1) * 128], pp[:, wb, cs], False, last))
                                K.op(dve, [R("pp", wb), R("acc5", 0)], [R("acc5", 0)], lambda wb=wb, cs=cs: nc.vector.tensor_tensor(
                                    out=acc5[:, 0, cs], in0=acc5[:, 0, cs], in1=pp[:, wb, cs], op=ALU.add))
                        K.op(act, [R("acc5", 0)], [R("hl5", 0)], lambda: nc.scalar.copy(out=hl5[:, 0, :], in_=acc5[:, 0, :]))
                        K.op(dve, [R("acc5", 0), R("hl5", 0)], [R("hl5", 1)], lambda: nc.vector.tensor_tensor(
                            out=hl5[:, 1, :], in0=acc5[:, 0, :], in1=hl5[:, 0, :], op=ALU.subtract))
                        K.op(pe, [R("hl5", 0), R("hl5", 1), r_const], [RPS[RB]], lambda: (
                            mm(PS[RB][:], ones[:], hl5[:, 0, :], True, False), mm(PS[RB][:], ones[:], hl5[:, 1, :], False, True))[1])
                        K.op(dve, [RPS[RB]], [R("rr")], lambda: nc.vector.reciprocal(out=rr[:], in_=PS[RB][:]))
                        os_ = oi[0] % 2
                        oi[0] += 1
                        K.op(dve, [RPS[OB], R("rr")], [R("ost5", os_)], lambda os_=os_: nc.vector.tensor_tensor(
                            out=ost[:, os_, :], in0=PS[OB][:], in1=rr[:], op=ALU.mult))
                        K.dma(sp, [R("ost5", os_)], [R("att", h, G)], lambda os_=os_, h=h, G=G: nc.sync.dma_start(
                            out=att_d[h][:, G * 512:(G + 1) * 512], in_=ost[:, os_, :]))
            K.barrier()

    MINI = os.environ.get("KMINI", "")
    if not MINI:
        phase_p1()
    if debug and not MINI:
        dk = nc.dram_tensor("dbg_kx", [4, 1024, T], BF16, kind="ExternalOutput").ap()
        dv = nc.dram_tensor("dbg_v", [4, 1024, 2048], BF16, kind="ExternalOutput").ap()
        for i in range(4):
            K.dma(sp, [R("kx_all", i)], [R("dbg_kx")], lambda i=i: nc.sync.dma_start(out=dk[i], in_=kx_all[i]))
            K.dma(sp, [R("v_all", i)], [R("dbg_v")], lambda i=i: nc.sync.dma_start(out=dv[i], in_=v_all[i]))
    if stop_after >= 2:
        phase_p2()
    if stop_after >= 3:
        phase_mlp(0, wo0, x_own, True, None, xa_d, True)
    if stop_after >= 4:
        phase_p4()
    if debug and stop_after >= 4:
        dm = nc.dram_tensor("dbg_mx", [1152, T], BF16, kind="ExternalOutput").ap()
        K.dma(sp, [R("mx_all", 0)], [R("dbg_mx")], lambda: nc.sync.dma_start(out=dm[0:1024, :], in_=mx_all[0]))
        K.dma(sp, [R("mx_all", 1)], [R("dbg_mx")], lambda: nc.sync.dma_start(out=dm[1024:1152, :], in_=mx_all[1]))
    if stop_after >= 5:
        phase_p5()
    if stop_after >= 6:
        phase_mlp(1, mwo, xa_d, False, None, y_out, False)
    K.barrier()
    nc._used_inputs = used_inputs
    return nc


def _bf(a):
    return np.ascontiguousarray(a).astype(ml_dtypes.bfloat16)


def make_core_inputs(inputs, b, r, shared):
    x = inputs["x"][b]
    posb = inputs["positions"][b]
    idx = np.concatenate([np.arange(gtile(r, j) * 128, gtile(r, j) * 128 + 128) for j in range(NT)])
    d = dict(shared)
    d["x_own"] = np.ascontiguousarray(x[idx])
    d["pos"] = np.ascontiguousarray(posb[idx].reshape(1, T).astype(np.int32))
    d["masks"] = shared["_masks"][r]
    del d["_masks"]
    return d, idx


def make_shared(inputs):
    f32 = np.float32
    sh = {}
    masks = []
    s_ = np.arange(128)[:, None]
    for r in range(2):
        m = np.zeros((128, 2, 8, 512), f32)
        for rel in range(8):
            for jj in range(4):
                gq = 2 * jj + ((jj & 1) ^ r)
                tq = np.arange(128)[None, :]
                if rel < gq:
                    ns = np.ones((128, 128), f32)
                    st = ns
                elif rel == gq:
                    ns = (s_ <= tq).astype(f32)
                    st = (s_ < tq).astype(f32)
                else:
                    ns = np.zeros((128, 128), f32)
                    st = ns
                m[:, 0, rel, jj * 128:(jj + 1) * 128] = ns
                m[:, 1, rel, jj * 128:(jj + 1) * 128] = st
        masks.append(_bf(m))
    sh["_masks"] = masks
    cst = np.zeros((128, 2, 128), f32)
    cst[:, 0, :] = np.eye(128)
    jj_, ss_ = np.arange(128)[:, None], np.arange(128)[None, :]
    cst[:, 1, :] = -(jj_ >= ss_).astype(f32)
    sh["cstb"] = _bf(cst)
    colc = np.zeros((128, 8), f32)
    inv128 = (f32(10000.0) ** (-np.arange(0, 128, 2, dtype=f32) / f32(128))).astype(f32)
    inv64 = (f32(10000.0) ** (-np.arange(0, 64, 2, dtype=f32) / f32(64))).astype(f32)
    p = np.arange(128)
    colc[:, 0] = inv128[p % 64]
    colc[:, 1] = np.where(p < 64, -1.0, 1.0)
    colc[:64, 2] = inv64[p[:64] % 32]
    colc[:64, 3] = np.where(p[:64] < 32, -1.0, 1.0)
    sh["colc"] = colc
    w0 = np.ascontiguousarray(inputs["sb_diff_w_in"][0])
    sh["w0"] = w0
    def rot(wc, hd):
        n = wc.shape[1] // hd
        w3 = wc.reshape(wc.shape[0], n, hd)
        return np.ascontiguousarray(np.concatenate([w3[:, :, hd // 2:], w3[:, :, :hd // 2]], axis=2).reshape(wc.shape[0], n * hd))
    sh["w0r"] = np.ascontiguousarray(np.concatenate([rot(w0[:, 3072:4096], 128), rot(w0[:, 4096:5120], 128)], axis=1))
    sh["wo0"] = np.ascontiguousarray(inputs["sb_diff_w_out"][0])
    sh["lamv"] = np.ascontiguousarray(np.concatenate([inputs["diff_lambda_q1"][0], inputs["diff_lambda_k1"][0],
                                                      inputs["diff_lambda_q2"][0], inputs["diff_lambda_k2"][0]]).reshape(1, 512).astype(f32))
    sh["subg"] = np.ascontiguousarray(inputs["diff_subln_g"][0].reshape(2, 128).T.astype(f32))
    mwin = np.ascontiguousarray(inputs["mla_w_in"][0])
    sh["mwin"] = mwin
    sh["mwinr"] = rot(mwin[:, 1024:1088], 64)
    sh["mg"] = np.ascontiguousarray(np.concatenate([inputs["mla_q_norm_g"][0], inputs["mla_kv_norm_g"][0]]).reshape(1, 1024).astype(f32))
    wq = inputs["mla_w_q_up"][0].reshape(512, 16, 192)
    wq_n = wq[:, :, :128].reshape(512, 2048)
    wq_p = np.ascontiguousarray(wq[:, :, 128:].reshape(512, 1024))
    sh["mwq"] = np.ascontiguousarray(np.concatenate([wq_n, wq_p], axis=1))
    sh["mwqr"] = rot(wq_p, 64)
    wkv = inputs["mla_w_kv_up"][0].reshape(512, 16, 256)
    sh["mwkv"] = np.ascontiguousarray(np.concatenate([wkv[:, :, :128].reshape(512, 2048), wkv[:, :, 128:].reshape(512, 2048)], axis=1))
    sh["mwo"] = np.ascontiguousarray(inputs["mla_w_out"][0])
    sh["fg"] = np.ascontiguousarray(inputs["ffn_w_gate"])
    sh["fu"] = np.ascontiguousarray(inputs["ffn_w_up"])
    sh["fd"] = np.ascontiguousarray(inputs["ffn_w_down"])
    sh["lng"] = np.ascontiguousarray(inputs["ln_g"].reshape(4, D).astype(f32))
    sh["lnb"] = np.ascontiguousarray(inputs["ln_b"].reshape(4, D).astype(f32))
    return sh


_NC_CACHE = {}


def kernel(**inputs):
    inputs = {k: np.asarray(v) for k, v in inputs.items()}
    stop_after = int(os.environ.get("KSTOP", "99"))
    debug = os.environ.get("KDEBUG", "0") == "1"
    key = (stop_after, debug)
    if key not in _NC_CACHE:
        _NC_CACHE[key] = build(stop_after, debug)
    nc = _NC_CACHE[key]
    shared = make_shared(inputs)
    in_maps, idxs = [], []
    for core in range(8):
        b, r = core // 2, core % 2
        d, idx = make_core_inputs(inputs, b, r, shared)
        d = {k: v for k, v in d.items() if k in nc._used_inputs}
        in_maps.append(d)
        idxs.append(idx)
    res = run_bass_kernel_spmd(nc, in_maps, core_ids=list(range(8)))
    out = np.zeros((4, S, D), np.float32)
    for core in range(8):
        b = core // 2
        out[b, idxs[core], :] = np.asarray(res.results[core]["y"])
    if debug:
        kernel.last = res
    return out
```
